# Optimizing a Trainium2 kernel written in Bass

```python
import jax, jax.numpy as jnp
from jax import lax
import numpy as np

D_MODEL = 1024
BATCH = 8
SEQ = 2048
DEPTH = 2
DEC_BATCH = 128
DEC_SEQ = 8
PAST_LEN = 16384
PAGE_SIZE = 128

N_EVEN = (DEPTH + 1) // 2
N_ODD = DEPTH // 2
W_A = D_MODEL // 2
W_B = D_MODEL // 2
W_C = D_MODEL
CONV_A_WIDTH = 3
CONV_B_WIDTH = 31
POOL_WINDOWS = (2, 4, 8, 16)
N_POOL_GROUPS = len(POOL_WINDOWS)
POOL_GROUP = W_C // N_POOL_GROUPS
POOL_BUF = max(POOL_WINDOWS) - 1
PLE_DIM = 256
EVEN_SPLITS = (W_A, W_A, W_A, W_A, W_B, W_B, W_B)
IN_EVEN = sum(EVEN_SPLITS)
IN_ODD = 2 * W_C
EPS = 1e-6

kernel_name = "hybrid_shortconv_conformer_pool_decoder_step"


def _rmsnorm(x, g):
    x32 = x.astype(jnp.float32)
    y = x32 * lax.rsqrt(jnp.mean(x32 * x32, axis=-1, keepdims=True) + EPS)
    return (y * g.astype(jnp.float32)).astype(x.dtype)


def _layernorm(x, g, b):
    x32 = x.astype(jnp.float32)
    mu = jnp.mean(x32, axis=-1, keepdims=True)
    var = jnp.mean(jnp.square(x32 - mu), axis=-1, keepdims=True)
    y = (x32 - mu) * lax.rsqrt(var + EPS)
    return (y * g.astype(jnp.float32) + b.astype(jnp.float32)).astype(x.dtype)


def _split(x, widths):
    idx = np.cumsum(widths)[:-1].tolist()
    return jnp.split(x, idx, axis=-1)


def _causal_dwconv(buf, v, w):
    k = w.shape[0]
    ext = jnp.concatenate([buf.astype(v.dtype), v], axis=1)
    out = lax.conv_general_dilated(
        ext, w[:, None, :].astype(v.dtype), window_strides=(1,), padding='VALID',
        dimension_numbers=('NWC', 'WIO', 'NWC'), feature_group_count=v.shape[-1])
    return out, ext[:, ext.shape[1] - (k - 1):]


def _causal_multiscale_pool(buf, u, start_pos, pool_map, pool_scale):
    bsz, t_len, _ = u.shape
    ext = jnp.concatenate([buf.astype(u.dtype), u], axis=1)
    p_len = buf.shape[1]
    cs = jnp.cumsum(ext.astype(jnp.float32), axis=1)
    cs = jnp.concatenate([jnp.zeros((bsz, 1, W_C), jnp.float32), cs], axis=1)
    pos = start_pos + jnp.arange(t_len, dtype=jnp.int32)
    groups = []
    for g, w in enumerate(POOL_WINDOWS):
        lo, hi = g * POOL_GROUP, (g + 1) * POOL_GROUP
        win = cs[:, p_len + 1:p_len + t_len + 1, lo:hi] - cs[:, p_len + 1 - w:p_len + t_len + 1 - w, lo:hi]
        cnt = jnp.minimum(pos + 1, w).astype(jnp.float32)[None, :, None]
        groups.append(win / cnt)
    pooled = jnp.stack(groups, axis=2)
    ug = u.astype(jnp.float32).reshape(bsz, t_len, N_POOL_GROUPS, POOL_GROUP)
    mixed = jnp.einsum('btgc,gcd->btgd', (pooled - ug).astype(u.dtype), pool_map.astype(u.dtype))
    y = mixed.reshape(bsz, t_len, W_C) * pool_scale.astype(u.dtype)
    return y, ext[:, ext.shape[1] - POOL_BUF:]


def _trunk(x, p, buf_a, buf_b, buf_pool, start_pos,
           norm_g, w_in_even, conv_a_w, conv_b_w, conv_b_bias, ln_b_gamma, ln_b_beta, w_out_even,
           w_in_odd, pool_map, pool_scale, w_out_odd, ple_proj, ple_gate, final_norm_g):
    h = x
    new_a, new_b, new_pool = [], [], []
    for i in range(DEPTH):
        hn = _rmsnorm(h, norm_g[i])
        if i % 2 == 0:
            j = i // 2
            proj = jnp.einsum('btd,de->bte', hn, w_in_even[j])
            a_bg, a_cg, a_x, a_z, b_val, b_gate, b_z = _split(proj, EVEN_SPLITS)
            conv_a, nb_a = _causal_dwconv(buf_a[j], a_cg * a_x, conv_a_w[j])
            y_a = a_bg * conv_a * jax.nn.silu(a_z)
            glu = b_val * jax.nn.sigmoid(b_gate)
            conv_b, nb_b = _causal_dwconv(buf_b[j], glu, conv_b_w[j])
            conv_b = conv_b + conv_b_bias[j]
            y_b = jax.nn.silu(_layernorm(conv_b, ln_b_gamma[j], ln_b_beta[j])) * jax.nn.silu(b_z)
            out = jnp.einsum('bte,ed->btd', jnp.concatenate([y_a, y_b], axis=-1), w_out_even[j])
            new_a.append(nb_a)
            new_b.append(nb_b)
        else:
            j = i // 2
            proj = jnp.einsum('btd,de->bte', hn, w_in_odd[j])
            u, z = _split(proj, (W_C, W_C))
            y_c, nb_p = _causal_multiscale_pool(buf_pool[j], u, start_pos, pool_map[j], pool_scale[j])
            out = jnp.einsum('bte,ed->btd', y_c * jax.nn.silu(z), w_out_odd[j])
            new_pool.append(nb_p)
        h = h + out
        pe = jnp.einsum('btk,kd->btd', p[i].astype(h.dtype), ple_proj[i])
        h = h + pe * jax.nn.sigmoid(jnp.einsum('btd,de->bte', h, ple_gate[i]))
    y = _rmsnorm(h, final_norm_g)
    return y, jnp.stack(new_a, 0), jnp.stack(new_b, 0), jnp.stack(new_pool, 0)


def setup_inputs(seed: int = 0) -> dict:
    key = jax.random.key(seed)
    ks = jax.random.split(key, 24)
    f = jnp.float32
    n = lambda k, s, sc=1.0: (jax.random.normal(k, s, f) * sc).astype(f)
    return {
        "x_prompt": n(ks[0], (BATCH, SEQ, D_MODEL)),
        "x_sample": n(ks[1], (DEC_BATCH, DEC_SEQ, D_MODEL)),
        "state_conv_a": n(ks[2], (N_EVEN, DEC_BATCH, CONV_A_WIDTH - 1, W_A)),
        "state_conv_b": n(ks[3], (N_EVEN, DEC_BATCH, CONV_B_WIDTH - 1, W_B)),
        "state_pool": n(ks[4], (N_ODD, DEC_BATCH, POOL_BUF, W_C)),
        "p_prompt": n(ks[5], (DEPTH, BATCH, SEQ, PLE_DIM)),
        "p_sample": n(ks[6], (DEPTH, DEC_BATCH, DEC_SEQ, PLE_DIM)),
        "norm_g": 1.0 + n(ks[7], (DEPTH, D_MODEL), 0.02),
        "w_in_even": n(ks[8], (N_EVEN, D_MODEL, IN_EVEN), D_MODEL ** -0.5),
        "conv_a_w": n(ks[9], (N_EVEN, CONV_A_WIDTH, W_A), CONV_A_WIDTH ** -0.5),
        "conv_b_w": n(ks[10], (N_EVEN, CONV_B_WIDTH, W_B), CONV_B_WIDTH ** -0.5),
        "conv_b_bias": n(ks[11], (N_EVEN, W_B), 0.02),
        "ln_b_gamma": 1.0 + n(ks[12], (N_EVEN, W_B), 0.02),
        "ln_b_beta": n(ks[13], (N_EVEN, W_B), 0.02),
        "w_out_even": n(ks[14], (N_EVEN, W_A + W_B, D_MODEL), (W_A + W_B) ** -0.5),
        "w_in_odd": n(ks[15], (N_ODD, D_MODEL, IN_ODD), D_MODEL ** -0.5),
        "pool_map": n(ks[16], (N_ODD, N_POOL_GROUPS, POOL_GROUP, POOL_GROUP), POOL_GROUP ** -0.5),
        "pool_scale": 1.0 + n(ks[17], (N_ODD, W_C), 0.1),
        "w_out_odd": n(ks[18], (N_ODD, W_C, D_MODEL), W_C ** -0.5),
        "ple_proj": n(ks[19], (DEPTH, PLE_DIM, D_MODEL), PLE_DIM ** -0.5),
        "ple_gate": n(ks[20], (DEPTH, D_MODEL, D_MODEL), D_MODEL ** -0.5),
        "final_norm_g": 1.0 + n(ks[21], (D_MODEL,), 0.02),
    }


def reference(x_prompt, x_sample, state_conv_a, state_conv_b, state_pool, p_prompt, p_sample,
              norm_g, w_in_even, conv_a_w, conv_b_w, conv_b_bias, ln_b_gamma, ln_b_beta, w_out_even,
              w_in_odd, pool_map, pool_scale, w_out_odd, ple_proj, ple_gate, final_norm_g):
    weights = (norm_g, w_in_even, conv_a_w, conv_b_w, conv_b_bias, ln_b_gamma, ln_b_beta, w_out_even,
               w_in_odd, pool_map, pool_scale, w_out_odd, ple_proj, ple_gate, final_norm_g)
    dt = x_prompt.dtype
    zero_a = jnp.zeros((N_EVEN, BATCH, CONV_A_WIDTH - 1, W_A), dt)
    zero_b = jnp.zeros((N_EVEN, BATCH, CONV_B_WIDTH - 1, W_B), dt)
    zero_p = jnp.zeros((N_ODD, BATCH, POOL_BUF, W_C), dt)
    y_prompt, na_p, nb_p, npool_p = _trunk(x_prompt, p_prompt, zero_a, zero_b, zero_p, 0, *weights)
    y_sample, na_s, nb_s, npool_s = _trunk(x_sample, p_sample, state_conv_a, state_conv_b, state_pool,
                                           PAST_LEN, *weights)
    return (y_prompt, y_sample, na_p, nb_p, npool_p, na_s, nb_s, npool_s)
```

```python
import contextlib
import sys
import numpy as np
import concourse.bass as bass
import concourse.mybir as mybir
from concourse.bass_utils import run_bass_kernel_spmd

F32, BF16 = mybir.dt.float32, mybir.dt.bfloat16
ALU = mybir.AluOpType
AF = mybir.ActivationFunctionType
EPS = 1e-6
NCORES = 8
NSLOT = 5
NTMP = 9
TW = 528


class Buf:
    __slots__ = ("name", "w", "r", "excl")

    def __init__(self, name, excl=False):
        self.name, self.w, self.r, self.excl = name, None, {}, excl


class Sched:
    ENGS = ("pe", "act", "dve", "pool", "sp")

    def __init__(self):
        self.ops = {e: [] for e in self.ENGS}
        self.cnt = {e: 0 for e in self.ENGS}
        self.waited = {e: {} for e in self.ENGS}
        self.dcnt = {}
        self.labels = {e: [] for e in self.ENGS}
        self.ctx = "setup"

    def _deps(self, eng, reads, writes):
        need = {}

        def add(ev):
            if ev is not None and need.get(ev[0], 0) < ev[1]:
                need[ev[0]] = ev[1]
        for b in reads:
            add(b.w)
            if b.excl:
                for k, v in b.r.items():
                    if k != eng:
                        add((k, v))
        for b in writes:
            add(b.w)
            for k, v in b.r.items():
                add((k, v))
        waits = []
        for k, v in need.items():
            if self.waited[eng].get(k, 0) >= v:
                continue
            self.waited[eng][k] = v
            waits.append((k, v))
        return waits

    def _mark(self, ev, reads, writes):
        for b in reads:
            if b.r.get(ev[0], 0) < ev[1]:
                b.r[ev[0]] = ev[1]
        for b in writes:
            b.w, b.r = ev, {}

    def op(self, eng, fn, reads=(), writes=()):
        waits = self._deps(eng, reads, writes)
        self.cnt[eng] += 1
        ev = (eng, self.cnt[eng])
        self.ops[eng].append((waits, fn, ev, 1))
        self.labels[eng].append(self.ctx + ":" + sys._getframe(1).f_code.co_name)
        self._mark(ev, reads, writes)

    def group(self, eng, subs, writes=()):
        self.cnt[eng] += 1
        ev = (eng, self.cnt[eng])
        lab = self.ctx + ":" + sys._getframe(1).f_code.co_name
        n = len(subs)
        for i, (fn, reads) in enumerate(subs):
            waits = self._deps(eng, reads, writes if i == 0 else ())
            last = (i == n - 1)
            self.ops[eng].append((waits, fn, ev if last else None, 1 if last else 0))
            self.labels[eng].append(lab)
            for b in reads:
                if b.r.get(ev[0], 0) < ev[1]:
                    b.r[ev[0]] = ev[1]
        for b in writes:
            b.w, b.r = ev, {}

    def dma(self, eng, key, fn, reads=(), writes=()):
        waits = self._deps(eng, reads, writes)
        self.dcnt[key] = self.dcnt.get(key, 0) + 16
        ev = (key, self.dcnt[key])
        self.ops[eng].append((waits, fn, ev, 16))
        self.labels[eng].append(self.ctx + ":dma:" + sys._getframe(1).f_code.co_name)
        self._mark(ev, reads, writes)

    def sem_keys(self):
        return list(self.ENGS) + list(self.dcnt.keys())

    def final_wait(self, eng):
        waits = []
        for k in self.ENGS:
            if k != eng and self.cnt[k] > self.waited[eng].get(k, 0):
                waits.append((k, self.cnt[k]))
        for k, v in self.dcnt.items():
            if v > self.waited[eng].get(k, 0):
                waits.append((k, v))
        self.ops[eng].append((waits, None, None, 0))

    def replay(self, eng, e, sems):
        for waits, fn, ev, inc in self.ops[eng]:
            for k, v in waits:
                e.wait_ge(sems[k], v)
            if fn is None:
                continue
            ins = fn(e)
            if ev is not None:
                ins.then_inc(sems[ev[0]], inc)


DBG = None


def build(NPT):
    SEQ = NPT * 512
    nc = bass.Bass("TRN2", target_bir_lowering=False)

    def din(name, shape):
        return nc.dram_tensor(name, list(shape), F32, kind="ExternalInput").ap()

    def dout(name, shape):
        return nc.dram_tensor(name, list(shape), F32, kind="ExternalOutput").ap()

    xp = din("xp", [SEQ, 1024]); xs = din("xs", [128, 1024])
    sta = din("sta", [32, 512]); stb = din("stb", [16, 30, 512]); stp = din("stp", [16, 15, 1024])
    pp = din("pp", [2, SEQ, 256]); psm = din("psm", [2, 128, 256])
    PA = din("PA", [37, 512]); PB = din("PB", [4, 1024])
    w_ie = din("w_ie", [1024, 3584]); w_oe = din("w_oe", [1024, 1024])
    w_io = din("w_io", [1024, 2048]); pmap = din("pmap", [1024, 256]); w_oo = din("w_oo", [1024, 1024])
    plp = din("plp", [512, 1024]); plg = din("plg", [2048, 1024])
    cst = din("cst", [128, 128 + 64])
    y_p = dout("y_p", [SEQ, 1024]); y_s = dout("y_s", [128, 1024])
    na_p = dout("na_p", [2, 512]); nb_p = dout("nb_p", [30, 512]); np_p = dout("np_p", [15, 1024])
    na_s = dout("na_s", [32, 512]); nb_s = dout("nb_s", [16, 30, 512]); np_s = dout("np_s", [16, 15, 1024])

    S = Sched()
    es = contextlib.ExitStack()

    def sb(name, shape, dt=F32):
        return es.enter_context(nc.sbuf_tensor(name, list(shape), dt))

    ring = [sb(f"ring{i}", [128, 8 * 512], BF16) for i in range(NSLOT)]
    ringB = [Buf(f"ring{i}") for i in range(NSLOT)]
    hT = sb("hT", [128, 8, 512]); hTB = [Buf(f"hT{c}") for c in range(8)]
    sq = sb("sq", [128, 8, 512], BF16); sqB = [Buf(f"sq{c}") for c in range(8)]
    hbf = sb("hbf", [128, 8, 512], BF16); hbfB = [Buf(f"hbf{c}") for c in range(8)]
    ycat = sb("ycat", [128, 8, 512], BF16); ycatB = [Buf(f"ycat{c}") for c in range(8)]
    NIO = 3
    xio = [sb(f"xio{i}", [128, 1024]) for i in range(NIO)]; xioB = [Buf(f"xio{i}") for i in range(NIO)]
    pin = [sb(f"pin{i}", [128, 2, 256]) for i in range(2)]; pinB = [Buf(f"pin{i}") for i in range(2)]
    pT = sb("pT", [128, 4, 512], BF16); pTB = [Buf(f"pT{l}") for l in range(2)]
    glu = sb("glu", [128, 4, 608], BF16); gluB = [Buf(f"glu{j}") for j in range(4)]
    cxe = sb("cxe", [128, 4, 514]); cxB = [Buf(f"cx{j}") for j in range(4)]
    ue = sb("ue", [128, 8, 528]); ueB = [Buf(f"ue{c}") for c in range(8)]
    diag = sb("diag", [128, 4, 31, 128], BF16); diagB = Buf("diag")
    xb = sb("xb", [128, 4, 512]); xbB = [Buf(f"xb{j}") for j in range(4)]
    szb = sb("szb", [128, 4, 512], BF16); szbB = [Buf(f"szb{j}") for j in range(4)]
    xbb = sb("xbb", [128, 4, 512], BF16); xbbB = [Buf(f"xbb{j}") for j in range(4)]
    sqb = sb("sqb", [128, 4, 512], BF16); sqbB = [Buf(f"sqb{j}") for j in range(4)]
    gl32 = sb("gl32", [128, 4, 128]); gl32B = [Buf(f"gl32{j}") for j in range(4)]
    lnm = sb("lnm", [128, 512]); lnmB = Buf("lnm"); lnr = sb("lnr", [128, 512]); lnrB = Buf("lnr")
    tmp = [sb(f"tmp{i}", [128, TW]) for i in range(NTMP)]; tmpB = [Buf(f"tmp{i}") for i in range(NTMP)]
    cst_sb = sb("cst_sb", [128, 192]); cstB = Buf("cst")
    PAs = xio[2][0:37, 0:512]; PBs = xio[1][0:4, :]; parB = Buf("parstage")
    pa = sb("pa", [128, 4, 37]); pb = sb("pb", [128, 8, 4]); g32 = sb("g32", [128, 8, 3]); parT = Buf("parT")
    wbh = sb("wbh", [128, 4, 31])
    identb = sb("identb", [128, 128], BF16); onesb = sb("onesb", [128, 128], BF16)
    c512 = sb("c512", [128, 128], BF16); constB = Buf("constb")
    sst = xio[2]; sstB = xioB[2]
    ident = cst_sb[:, 0:128]
    cnt_t = cst_sb[:, 128:192]

    banks = [es.enter_context(nc.psum_tensor(f"bank{i}", [128, 512], F32)) for i in range(8)]
    bankB = [Buf(f"bank{i}", excl=True) for i in range(8)]
    st_ = {"bank": 0, "tmp": 0, "io": 0, "pin": 0, "ioq": "sp"}

    def iodma(key, fn, reads=(), writes=()):
        q = st_["ioq"]
        S.dma(q, f"{key}_{q}", fn, reads=reads, writes=writes)

    def nbank():
        i = st_["bank"]; st_["bank"] = (i + 1) % 8
        return banks[i], bankB[i]

    def ntmp():
        i = st_["tmp"]; st_["tmp"] = (i + 1) % NTMP
        return tmp[i], tmpB[i]

    def nio():
        i = st_["io"]; st_["io"] = (i + 1) % 3
        return i

    def wview(w, ncols_total):
        return w.rearrange("(kc p) n -> p kc n", p=128)

    v_ie = w_ie.rearrange("(kc p) (s c) -> p kc s c", p=128, c=512)
    blocks = []

    def blk_list():
        L = []
        for nm, sp in (("Sv", 4), ("Sg", 5), ("Sz", 6), ("Sbg", 0), ("Scg", 1), ("Sx", 2), ("Saz", 3)):
            L.append((nm, wview(w_ie, 3584)[:, :, sp * 512:(sp + 1) * 512], 8, (512,)))
        for h in range(2):
            L.append((f"Woe{h}", wview(w_oe, 1024)[:, :, h * 512:(h + 1) * 512], 8, (512,)))
        def wp(l, h):
            return (f"Wp{l}{h}", plp[l * 256:(l + 1) * 256, :].rearrange("(kc p) n -> p kc n", p=128)[:, :, h * 512:(h + 1) * 512], 2, (512,))

        def wg(l, h):
            return (f"Wg{l}{h}", plg[l * 1024:(l + 1) * 1024, :].rearrange("(kc p) n -> p kc n", p=128)[:, :, h * 512:(h + 1) * 512], 8, (512,))
        for h in range(2):
            L.append(wp(0, h)); L.append(wg(0, h))
        pmv = pmap.rearrange("(gk p) n -> p gk n", p=128)
        for h in range(2):
            L.append((f"Bu{h}", wview(w_io, 2048)[:, :, h * 512:(h + 1) * 512], 8, (512,)))
            L.append((f"Bz{h}", wview(w_io, 2048)[:, :, 1024 + h * 512:1024 + (h + 1) * 512], 8, (512,)))
            L.append((f"Pm{h}", pmv[:, h * 4:(h + 1) * 4, :], 4, (256,)))
        for h in range(2):
            L.append((f"Woo{h}", wview(w_oo, 1024)[:, :, h * 512:(h + 1) * 512], 8, (512,)))
        for h in range(2):
            L.append(wp(1, h)); L.append(wg(1, h))
        return L

    NT = NPT + 1
    allblk = []
    for t in range(NT):
        for b in blk_list():
            allblk.append(b)
    wst = {"issued": 0, "use": 0}

    def slot_view(si, kc, fshape):
        n = int(np.prod(fshape))
        v = ring[si][:, 0:kc * n].rearrange("p (k n) -> p k n", n=n)
        if len(fshape) == 2:
            v = v.rearrange("p k (s c) -> p k s c", c=fshape[1])
        return v

    NB = len(blk_list())
    wscr = nc.dram_tensor("wscr", [NB, 128, 4096], BF16).ap()
    scrB = [Buf(f"scr{b}") for b in range(NB)]

    def issue_weights(upto):
        while wst["issued"] < min(upto, len(allblk)):
            bi = wst["issued"]; si = bi % NSLOT
            pass_, b = bi // NB, bi % NB
            name, src, kc, fshape = allblk[bi]
            n = int(np.prod(fshape))
            used = kc * n
            defer = (len(fshape) == 1 and kc == 8 and (b % 2 == 0))
            if pass_ == 0 or (pass_ == 1 and defer):
                dv = ring[si][:, 0:used].rearrange("p (k n) -> p k n", n=n)
                if len(fshape) == 2:
                    pairs = [(dv[:, :, s_ * fshape[1]:(s_ + 1) * fshape[1]], src[:, :, s_, :]) for s_ in range(fshape[0])]
                else:
                    pairs = [(dv, src)]
                for dst, sr in pairs:
                    S.dma("pool", f"ring{si}", lambda e, dst=dst, sr=sr: e.dma_start(out=dst, in_=sr),
                          reads=(), writes=(ringB[si],))
                if (pass_ == 0) != defer:
                    S.dma("sp", f"wb{si}", lambda e, b=b, si=si, used=used: e.dma_start(out=wscr[b, :, 0:used], in_=ring[si][:, 0:used]),
                          reads=(ringB[si],), writes=(scrB[b],))
            else:
                S.dma("sp", f"ringh{si}", lambda e, b=b, si=si, used=used: e.dma_start(out=ring[si][:, 0:used], in_=wscr[b, :, 0:used]),
                      reads=(scrB[b],), writes=(ringB[si],))
            wst["issued"] += 1

    def use_group(names):
        bi0 = wst["use"]
        issue_weights(bi0 + NSLOT)
        out = []
        for i, expect in enumerate(names):
            bi = bi0 + i; si = bi % NSLOT
            name, src, kc, fshape = allblk[bi]
            assert name == expect, (name, expect)
            n = int(np.prod(fshape))
            v = ring[si][:, 0:kc * n].rearrange("p (k n) -> p k n", n=n)
            out.append(((lambda k, c0, v=v: v[:, k, c0:c0 + 128]), ringB[si]))
        wst["use"] += len(names)
        return out

    def use_block(expect):
        return use_group([expect])[0]

    S.dma("sp", "ld_cst", lambda e: e.dma_start(out=cst_sb[:], in_=cst), writes=(cstB,))
    S.dma("sp", "ld_par", lambda e: e.dma_start(out=PAs, in_=PA), writes=(parB, xioB[2]))
    S.dma("sp", "ld_par", lambda e: e.dma_start(out=PBs, in_=PB), writes=(parB, xioB[1]))
    S.op("dve", lambda e: e.tensor_copy(out=identb[:], in_=ident), reads=(cstB,), writes=(constB,))
    S.op("dve", lambda e: e.memset(onesb[:], 1.0), writes=(constB,))
    S.op("dve", lambda e: e.memset(c512[:], 1.0 / 512.0), writes=(constB,))
    S.op("dve", lambda e: e.memset(cxe[:, :, 0:2], 0.0), writes=tuple(cxB))
    S.op("dve", lambda e: e.memset(glu[:, :, 0:30], 0.0), writes=tuple(gluB))
    S.op("dve", lambda e: e.memset(ue[:, :, 0:15], 0.0), writes=tuple(ueB))
    bk, bkB = nbank()

    def _tp_pa(e):
        for j in range(4):
            i = e.transpose(out=bk[:, j * 37:(j + 1) * 37], in_=PAs[:, j * 128:(j + 1) * 128], identity=ident[0:37, 0:37])
        for c in range(8):
            i = e.transpose(out=bk[:, 160 + c * 4:160 + (c + 1) * 4], in_=PBs[:, c * 128:(c + 1) * 128], identity=ident[0:4, 0:4])
        return i
    S.op("pe", _tp_pa, reads=(parB, cstB, xioB[1], xioB[2]), writes=(bkB,))
    S.op("dve", lambda e: e.tensor_copy(out=pa[:].rearrange("p j r -> p (j r)"), in_=bk[:, 0:148]), reads=(bkB,), writes=(parT,))
    S.op("dve", lambda e: e.tensor_copy(out=pb[:].rearrange("p c r -> p (c r)"), in_=bk[:, 160:192]), reads=(bkB,), writes=(parT,))
    S.op("dve", lambda e: e.tensor_scalar(out=g32[:, :, 0:2], in0=pb[:, :, 0:2], scalar1=1.0, scalar2=None, op0=ALU.mult), reads=(parT,), writes=(parT,))
    S.op("dve", lambda e: e.tensor_scalar(out=g32[:, :, 2:3], in0=pb[:, :, 3:4], scalar1=1.0, scalar2=None, op0=ALU.mult), reads=(parT,), writes=(parT,))
    S.op("dve", lambda e: e.tensor_scalar(out=wbh[:], in0=pa[:, :, 3:34], scalar1=0.5, scalar2=None, op0=ALU.mult), reads=(parT,), writes=(parT,))
    diagBj = [Buf(f"diag{j}") for j in range(4)]

    def build_diag(j):
        S.op("pool", lambda e, j=j: e.tensor_tensor(out=diag[:, j, :, :], in0=identb[:].unsqueeze(1).broadcast_to([128, 31, 128]),
                                                    in1=wbh[:, j, :].unsqueeze(2).broadcast_to([128, 31, 128]), op=ALU.mult),
             reads=(parT, constB), writes=(diagBj[j],))

    cawv = lambda j, k: pa[:, j, k:k + 1]
    biasv = lambda j: pa[:, j, 34:35]
    gamv = lambda j: pa[:, j, 35:36]
    betv = lambda j: pa[:, j, 36:37]
    pscv = lambda c: pb[:, c, 2:3]

    class Seg:
        def __init__(self, kind, ti):
            self.kind, self.ti = kind, ti
            self.T = 512 if kind == "P" else 128
            self.nsub = self.T // 128

        def v(self, ap):
            return ap if self.kind == "P" else ap.rearrange("p (s t) -> p s t", t=8)

        def tap(self, ext, c, H, k):
            if self.kind == "P":
                return ext[:, c, k:k + 512]
            L = H + 8
            return ext[:, c, 0:16 * L].rearrange("p (s l) -> p s l", l=L)[:, :, k:k + 8]

        def new(self, ext, c, H):
            return self.tap(ext, c, H, H)

    def mm_group(e, out_ap, pairs):
        n = len(pairs)
        for i, (l, r) in enumerate(pairs):
            ins = e.matmul(out_ap, lhsT=l, rhs=r, start=(i == 0), stop=(i == n - 1))
        return ins

    def load_tile(sg):
        T = sg.T
        xbst = xb[:].rearrange("p a b -> p (a b)")[:, 0:1024]
        for sub in range(sg.nsub):
            if sg.kind == "P":
                r0 = sg.ti * 512 + sub * 128
                xsrc = xp[r0:r0 + 128, :]; psrc = pp[:, r0:r0 + 128, :].rearrange("l t d -> t l d")
            else:
                xsrc = xs; psrc = psm.rearrange("l t d -> t l d")
            if sg.kind == "P" and sub == 3:
                stg, stgB = xbst, (xbB[0], xbB[1])
                iodma("xbst", lambda e, xsrc=xsrc: e.dma_start(out=xbst, in_=xsrc), writes=stgB)
            else:
                io = nio()
                stg, stgB = xio[io], (xioB[io],)
                iodma(f"xio{io}", lambda e, io=io, xsrc=xsrc: e.dma_start(out=xio[io][:], in_=xsrc), writes=stgB)
            pi = st_["pin"]; st_["pin"] = (pi + 1) % 2
            iodma(f"pin{pi}", lambda e, pi=pi, psrc=psrc: e.dma_start(out=pin[pi][:], in_=psrc), writes=(pinB[pi],))
            for half in range(2):
                bk, bkB = nbank()

                def _tp(e, bk=bk, stg=stg, half=half):
                    for q in range(4):
                        fc = half * 4 + q
                        i = e.transpose(out=bk[:, q * 128:(q + 1) * 128], in_=stg[:, fc * 128:(fc + 1) * 128], identity=ident)
                    return i
                S.op("pe", _tp, reads=stgB + (cstB,), writes=(bkB,))
                src3 = bk[:, :].rearrange("p (q t) -> p q t", t=128)
                S.op("dve", lambda e, half=half, sub=sub, src3=src3: e.tensor_copy(out=hT[:, half * 4:half * 4 + 4, sub * 128:(sub + 1) * 128], in_=src3),
                     reads=(bkB,), writes=tuple(hTB[half * 4:half * 4 + 4]))
                S.op("act", lambda e, half=half, sub=sub, src3=src3: e.activation(out=sq[:, half * 4:half * 4 + 4, sub * 128:(sub + 1) * 128], in_=src3, func=AF.Square),
                     reads=(bkB,), writes=tuple(sqB[half * 4:half * 4 + 4]))
            bk, bkB = nbank()

            def _tpp(e, bk=bk, pi=pi):
                for l in range(2):
                    for kc in range(2):
                        q = l * 2 + kc
                        i = e.transpose(out=bk[:, q * 128:(q + 1) * 128], in_=pin[pi][:, l, kc * 128:(kc + 1) * 128], identity=ident)
                return i
            S.op("pe", _tpp, reads=(pinB[pi], cstB), writes=(bkB,))
            S.op("act", lambda e, bk=bk, sub=sub: e.activation(out=pT[:, :, sub * 128:(sub + 1) * 128],
                                                               in_=bk[:, :].rearrange("p (q t) -> p q t", t=128), func=AF.Copy),
                 reads=(bkB,), writes=tuple(pTB))

    def make_hu0(sg):
        T = sg.T
        for c in range(8):
            S.op("act", lambda e, c=c: e.activation(out=hu0[:, c, 0:T], in_=hT[:, c, 0:T], func=AF.Copy, scale=g32[:, c, 0:1]),
                 reads=(hTB[c], parT), writes=(hu0B[c],))

    def rms_norm_in(sg, gi):
        T = sg.T
        bk, bkB = nbank()
        S.group("pe", [(lambda e, c=c: e.matmul(bk[:, 0:T], lhsT=onesb[:], rhs=sq[:, c, 0:T], start=(c == 0), stop=(c == 7)),
                        (sqB[c], constB)) for c in range(8)], writes=(bkB,))
        t1, t1B = ntmp()
        S.op("act", lambda e: e.activation(out=t1[:, 0:T], in_=bk[:, 0:T], func=AF.Sqrt, bias=EPS, scale=1.0 / 1024.0), reads=(bkB,), writes=(t1B,))
        rr, rrB = ntmp()
        S.op("dve", lambda e: e.reciprocal(out=rr[:, 0:T], in_=t1[:, 0:T]), reads=(t1B,), writes=(rrB,))
        return rr, rrB

    def layer_in(sg, gi, pre=None):
        T = sg.T
        rr, rrB = pre if pre is not None else rms_norm_in(sg, gi)
        for c in range(8):
            S.op("dve", lambda e, c=c: e.scalar_tensor_tensor(out=hbf[:, c, 0:T], in0=hT[:, c, 0:T], scalar=g32[:, c, gi:gi + 1], in1=rr[:, 0:T],
                                                            op0=ALU.mult, op1=ALU.mult),
                 reads=(hTB[c], rrB, parT), writes=(hbfB[c],))
        return rr, rrB

    def proj_chunk(sg, lhs, wB, col0, src=None, srcB=None, kcs=range(8)):
        T = sg.T
        src = hbf if src is None else src
        srcB = hbfB if srcB is None else srcB
        bk, bkB = nbank()
        kl = list(kcs)
        n = len(kl)
        S.group("pe", [(lambda e, i=i, k=k: e.matmul(bk[:, 0:T], lhsT=lhs(k, col0), rhs=src[:, k, 0:T], start=(i == 0), stop=(i == n - 1)),
                        (wB, srcB[k])) for i, k in enumerate(kl)], writes=(bkB,))
        return bk, bkB

    def proj_multi(sg, specs, src=None, srcB=None, korder=range(8)):
        T = sg.T
        src = hbf if src is None else src
        srcB = hbfB if srcB is None else srcB
        bks = [nbank() for _ in specs]
        subs = []
        kl = list(korder)
        for i, k in enumerate(kl):
            for (bk, bkB), (lhs, wB, c0) in zip(bks, specs):
                subs.append((lambda e, bk=bk, c0=c0, k=k, lhs=lhs, i=i: e.matmul(bk[:, 0:T], lhsT=lhs(k, c0), rhs=src[:, k, 0:T], start=(i == 0), stop=(i == len(kl) - 1)),
                             (wB, srcB[k])))
        S.group("pe", subs, writes=tuple(b for _, b in bks))
        return bks

    def mixerB_front(sg, j, last, grp, rrp=None):
        T = sg.T
        (lv, wvB), (lg_, wgB_), (lz_, wzB_) = grp
        if sg.ti == 0 and sg.kind == "P" and j == 0:
            for jj in range(4):
                build_diag(jj)
        if j in (0, 1):
            (bv, bvB), (bg, bgB), (bz, bzB) = proj_multi(sg, [(lv, wvB, j * 128), (lg_, wgB_, j * 128), (lz_, wzB_, j * 128)], src=hu0, srcB=hu0B)
            rr, rrB = rrp
            outs = []
            for (pb_, pbB) in ((bg, bgB), (bv, bvB), (bz, bzB)):
                t_, tB_ = ntmp()
                S.op("dve", lambda e, t_=t_, pb_=pb_: e.tensor_tensor(out=t_[:, 0:T], in0=pb_[:, 0:T], in1=rr[:, 0:T], op=ALU.mult), reads=(pbB, rrB), writes=(tB_,))
                outs.append((t_, tB_))
            (bg, bgB), (bv, bvB), (bz, bzB) = outs
        else:
            bv, bvB = proj_chunk(sg, lv, wvB, j * 128)
            bg, bgB = proj_chunk(sg, lg_, wgB_, j * 128)
            bz, bzB = proj_chunk(sg, lz_, wzB_, j * 128)
        th, thB = ntmp()
        S.op("act", lambda e: e.activation(out=th[:, 0:T], in_=bg[:, 0:T], func=AF.Tanh, scale=0.5), reads=(bgB,), writes=(thB,))
        S.op("dve", lambda e: e.scalar_tensor_tensor(out=sg.new(glu, j, 30), in0=sg.v(th[:, 0:T]), scalar=1.0, in1=sg.v(bv[:, 0:T]), op0=ALU.add, op1=ALU.mult),
             reads=(thB, bvB), writes=(gluB[j],))
        if last:
            n = 32 if sg.kind == "P" else 128
            S.op("dve", lambda e: e.scalar_tensor_tensor(out=gl32[:, j, 0:n], in0=th[:, T - n:T], scalar=1.0, in1=bv[:, T - n:T], op0=ALU.add, op1=ALU.mult),
                 reads=(thB, bvB), writes=(gl32B[j],))
        S.op("act", lambda e: e.activation(out=szb[:, j, 0:T], in_=bz[:, 0:T], func=AF.Silu), reads=(bzB,), writes=(szbB[j],))

    def mixerB_conv(sg, j):
        T = sg.T
        bk, bkB = nbank()
        S.op("pe", lambda e: mm_group(e, sg.v(bk[:, 0:T]), [(diag[:, j, k, :], sg.tap(glu, j, 30, k)) for k in range(31)]),
             reads=(diagBj[j], gluB[j]), writes=(bkB,))
        S.op("act", lambda e: e.activation(out=xb[:, j, 0:T], in_=bk[:, 0:T], func=AF.Identity, bias=biasv(j)), reads=(bkB, parT), writes=(xbB[j],))
        S.op("act", lambda e: e.activation(out=sqb[:, j, 0:T], in_=bk[:, 0:T], func=AF.Square, bias=biasv(j)), reads=(bkB, parT), writes=(sqbB[j],))
        S.op("dve", lambda e: e.tensor_copy(out=xbb[:, j, 0:T], in_=xb[:, j, 0:T]), reads=(xbB[j],), writes=(xbbB[j],))

    def mixerB_stats(sg):
        T = sg.T
        bm, bmB = nbank()
        S.op("pe", lambda e: mm_group(e, bm[:, 0:T], [(c512[:], xbb[:, j, 0:T]) for j in range(4)]), reads=tuple(xbbB) + (constB,), writes=(bmB,))
        bq, bqB = nbank()
        S.op("pe", lambda e: mm_group(e, bq[:, 0:T], [(c512[:], sqb[:, j, 0:T]) for j in range(4)]), reads=tuple(sqbB) + (constB,), writes=(bqB,))
        mean, meanB = lnm, lnmB
        S.op("act", lambda e: e.activation(out=mean[:, 0:T], in_=bm[:, 0:T], func=AF.Copy), reads=(bmB,), writes=(meanB,))
        m2, m2B = ntmp()
        S.op("dve", lambda e: e.tensor_tensor(out=m2[:, 0:T], in0=mean[:, 0:T], in1=mean[:, 0:T], op=ALU.mult), reads=(meanB,), writes=(m2B,))
        var, varB = ntmp()
        S.op("dve", lambda e: e.scalar_tensor_tensor(out=var[:, 0:T], in0=bq[:, 0:T], scalar=EPS, in1=m2[:, 0:T], op0=ALU.add, op1=ALU.subtract), reads=(bqB, m2B), writes=(varB,))
        rs, rsB = lnr, lnrB
        sd, sdB = ntmp()
        S.op("act", lambda e: e.activation(out=sd[:, 0:T], in_=var[:, 0:T], func=AF.Sqrt), reads=(varB,), writes=(sdB,))
        S.op("dve", lambda e: e.reciprocal(out=rs[:, 0:T], in_=sd[:, 0:T]), reads=(sdB,), writes=(rsB,))
        return mean, meanB, rs, rsB

    def mixerB_back_a(sg, j, mean, meanB, rs, rsB):
        T = sg.T
        xc, xcB = ntmp()
        S.op("dve", lambda e: e.tensor_tensor(out=xc[:, 0:T], in0=xb[:, j, 0:T], in1=mean[:, 0:T], op=ALU.subtract), reads=(xbB[j], meanB), writes=(xcB,))
        xn, xnB = ntmp()
        S.op("dve", lambda e: e.tensor_tensor(out=xn[:, 0:T], in0=xc[:, 0:T], in1=rs[:, 0:T], op=ALU.mult), reads=(xcB, rsB), writes=(xnB,))
        s_, sB = ntmp()
        S.op("act", lambda e: e.activation(out=s_[:, 0:T], in_=xn[:, 0:T], func=AF.Silu, bias=betv(j), scale=gamv(j)), reads=(xnB, parT), writes=(sB,))
        return s_, sB

    def mixerB_back_b(sg, j, s_, sB):
        T = sg.T
        S.op("dve", lambda e: e.tensor_tensor(out=ycat[:, 4 + j, 0:T], in0=s_[:, 0:T], in1=szb[:, j, 0:T], op=ALU.mult), reads=(sB, szbB[j]), writes=(ycatB[4 + j],))

    def mixerA(sg, j, grp):
        T = sg.T
        (lbg, wbgB), (lcg, wcgB), (lx, wxB), (laz, wazB) = grp
        cg, cgB = proj_chunk(sg, lcg, wcgB, j * 128)
        ax, axB = proj_chunk(sg, lx, wxB, j * 128)
        bgk, bgkB = proj_chunk(sg, lbg, wbgB, j * 128)
        az, azB = proj_chunk(sg, laz, wazB, j * 128)
        cgs, cgsB = ntmp()
        S.op("act", lambda e: e.activation(out=cgs[:, 0:T], in_=cg[:, 0:T], func=AF.Copy), reads=(cgB,), writes=(cgsB,))
        S.op("dve", lambda e: e.tensor_tensor(out=sg.new(cxe, j, 2), in0=sg.v(ax[:, 0:T]), in1=sg.v(cgs[:, 0:T]), op=ALU.mult), reads=(axB, cgsB), writes=(cxB[j],))
        a0, a0B = ntmp()
        S.op("dve", lambda e: e.tensor_scalar(out=sg.v(a0[:, 0:T]), in0=sg.tap(cxe, j, 2, 0), scalar1=cawv(j, 0), scalar2=None, op0=ALU.mult), reads=(cxB[j], parT), writes=(a0B,))
        a1, a1B = ntmp()
        S.op("dve", lambda e: e.scalar_tensor_tensor(out=sg.v(a1[:, 0:T]), in0=sg.tap(cxe, j, 2, 1), scalar=cawv(j, 1), in1=sg.v(a0[:, 0:T]), op0=ALU.mult, op1=ALU.add),
             reads=(cxB[j], parT, a0B), writes=(a1B,))
        a2, a2B = ntmp()
        S.op("dve", lambda e: e.scalar_tensor_tensor(out=sg.v(a2[:, 0:T]), in0=sg.tap(cxe, j, 2, 2), scalar=cawv(j, 2), in1=sg.v(a1[:, 0:T]), op0=ALU.mult, op1=ALU.add),
             reads=(cxB[j], parT, a1B), writes=(a2B,))
        sz, szB = ntmp()
        S.op("act", lambda e: e.activation(out=sz[:, 0:T], in_=az[:, 0:T], func=AF.Silu), reads=(azB,), writes=(szB,))
        t1, t1B = ntmp()
        S.op("dve", lambda e: e.tensor_tensor(out=t1[:, 0:T], in0=bgk[:, 0:T], in1=a2[:, 0:T], op=ALU.mult), reads=(bgkB, a2B), writes=(t1B,))
        S.op("dve", lambda e: e.tensor_tensor(out=ycat[:, j, 0:T], in0=t1[:, 0:T], in1=sz[:, 0:T], op=ALU.mult), reads=(t1B, szB), writes=(ycatB[j],))

    def out_proj(sg, names, korder):
        T = sg.T
        for h in range(2):
            lhs, wB = use_block(names[h])
            pm_ = proj_multi(sg, [(lhs, wB, q_ * 128) for q_ in range(4)], src=ycat, srcB=ycatB, korder=korder) if h == 0 else None
            for q in range(4):
                oc = h * 4 + q
                bk, bkB = pm_[q] if pm_ is not None else proj_chunk(sg, lhs, wB, q * 128, src=ycat, srcB=ycatB, kcs=korder)
                S.op("dve", lambda e, oc=oc, bk=bk: e.tensor_tensor(out=hT[:, oc, 0:T], in0=bk[:, 0:T], in1=hT[:, oc, 0:T], op=ALU.add), reads=(bkB, hTB[oc]), writes=(hTB[oc],))
                S.op("act", lambda e, oc=oc: e.activation(out=hbf[:, oc, 0:T], in_=hT[:, oc, 0:T], func=AF.Copy), reads=(hTB[oc],), writes=(hbfB[oc],))

    def ple(sg, l):
        T = sg.T
        pend = None
        for h in range(2):
            (lp, wpB), (lg, wgB) = use_group([f"Wp{l}{h}", f"Wg{l}{h}"])
            for q in range(4):
                oc = h * 4 + q
                bg, bgB = proj_chunk(sg, lg, wgB, q * 128)
                bp, bpB = nbank()
                S.op("pe", lambda e, bp=bp, q=q, lp=lp: mm_group(e, bp[:, 0:T], [(lp(k, q * 128), pT[:, l * 2 + k, 0:T]) for k in range(2)]),
                     reads=(wpB, pTB[l]), writes=(bpB,))
                th, thB = ntmp()
                S.op("act", lambda e, th=th, bg=bg: e.activation(out=th[:, 0:T], in_=bg[:, 0:T], func=AF.Tanh, scale=0.5), reads=(bgB,), writes=(thB,))
                if pend is not None:
                    S.op("act", lambda e, oc=pend: e.activation(out=sq[:, oc, 0:T], in_=hT[:, oc, 0:T], func=AF.Square), reads=(hTB[pend],), writes=(sqB[pend],))
                    if l == 0:
                        S.op("act", lambda e, oc=pend: e.activation(out=ycat[:, oc, 0:T], in_=hT[:, oc, 0:T], func=AF.Copy, scale=g32[:, oc, 1:2]),
                             reads=(hTB[pend], parT), writes=(ycatB[pend],))
                t, tB = ntmp()
                S.op("dve", lambda e, t=t, th=th, bp=bp: e.scalar_tensor_tensor(out=t[:, 0:T], in0=th[:, 0:T], scalar=1.0, in1=bp[:, 0:T], op0=ALU.add, op1=ALU.mult),
                     reads=(thB, bpB), writes=(tB,))
                S.op("dve", lambda e, t=t, oc=oc: e.scalar_tensor_tensor(out=hT[:, oc, 0:T], in0=t[:, 0:T], scalar=0.5, in1=hT[:, oc, 0:T], op0=ALU.mult, op1=ALU.add),
                     reads=(tB, hTB[oc]), writes=(hTB[oc],))
                pend = oc
        S.op("act", lambda e, oc=pend: e.activation(out=sq[:, oc, 0:T], in_=hT[:, oc, 0:T], func=AF.Square), reads=(hTB[pend],), writes=(sqB[pend],))
        if l == 0:
            S.op("act", lambda e, oc=pend: e.activation(out=ycat[:, oc, 0:T], in_=hT[:, oc, 0:T], func=AF.Copy, scale=g32[:, oc, 1:2]),
                 reads=(hTB[pend], parT), writes=(ycatB[pend],))

    def layer0(sg, last):
        rrp = layer_in(sg, 0, pre=sg.rr0)
        gB = use_group(["Sv", "Sg", "Sz"])
        mixerB_front(sg, 0, last, gB, rrp)
        mixerB_front(sg, 1, last, gB, rrp)
        mixerB_front(sg, 2, last, gB)
        mixerB_conv(sg, 0)
        mixerB_front(sg, 3, last, gB)
        mixerB_conv(sg, 1)
        mixerB_conv(sg, 2)
        mixerB_conv(sg, 3)
        gA = use_group(["Sbg", "Scg", "Sx", "Saz"])
        mixerA(sg, 0, gA)
        st = mixerB_stats(sg)
        prev = None
        for j in range(4):
            cur = mixerB_back_a(sg, j, *st)
            if prev is not None:
                mixerB_back_b(sg, j - 1, *prev)
            prev = cur
        mixerB_back_b(sg, 3, *prev)
        for j in range(1, 4):
            mixerA(sg, j, gA)
        if DBG == "ycat":
            return
        out_proj(sg, ("Woe0", "Woe1"), [4, 5, 6, 7, 0, 1, 2, 3])
        if DBG == "mixer":
            return
        ple(sg, 0)

    def layer1(sg, first, hook=None):
        T = sg.T
        L = 15 + 512 if sg.kind == "P" else 16 * 23
        rr1, rr1B = layer_in(sg, 1)
        for h in range(2):
            lu, wuB = use_block(f"Bu{h}")
            ubk = proj_multi(sg, [(lu, wuB, q_ * 128) for q_ in range(4)], src=ycat, srcB=ycatB) if h == 0 else None
            for q in range(4):
                c = h * 4 + q
                bk, bkB = ubk[q] if ubk is not None else proj_chunk(sg, lu, wuB, q * 128)
                if ubk is not None:
                    S.op("dve", lambda e, bk=bk, c=c: e.tensor_tensor(out=sg.new(ue, c, 15), in0=sg.v(bk[:, 0:T]), in1=sg.v(rr1[:, 0:T]), op=ALU.mult),
                         reads=(bkB, rr1B), writes=(ueB[c],))
                else:
                    S.op("act", lambda e, bk=bk, c=c: e.activation(out=sg.new(ue, c, 15), in_=sg.v(bk[:, 0:T]), func=AF.Copy), reads=(bkB,), writes=(ueB[c],))
            lz, wzB = use_block(f"Bz{h}")
            for q in range(4):
                c = h * 4 + q
                g = c // 2
                w = 2 << g
                cur, curB, lo = ue[:, c, :], ueB[c], 0
                for lvl in range(g + 1):
                    sh = 1 << lvl
                    nlo = lo + sh
                    nt_, ntB = ntmp()
                    S.op("pool" if c < 4 else "dve", lambda e, nt_=nt_, cur=cur, nlo=nlo, sh=sh: e.tensor_tensor(out=nt_[:, nlo:L], in0=cur[:, nlo:L], in1=cur[:, nlo - sh:L - sh], op=ALU.add),
                         reads=(curB,), writes=(ntB,))
                    cur, curB, lo = nt_, ntB, nlo
                if sg.kind == "P":
                    wn = cur[:, 15:15 + 512]
                else:
                    wn = cur[:, 0:16 * 23].rearrange("p (s l) -> p s l", l=23)[:, :, 15:23]
                S.op("dve", lambda e, wn=wn, c=c, w=w: e.scalar_tensor_tensor(out=sg.v(ycat[:, c, 0:T]), in0=wn, scalar=1.0 / float(w), in1=sg.new(ue, c, 15), op0=ALU.mult, op1=ALU.subtract),
                     reads=(curB, ueB[c]), writes=(ycatB[c],))
                if first:
                    f1, f1B = ntmp()
                    S.op("dve", lambda e, f1=f1, cur=cur, g=g: e.tensor_tensor(out=f1[:, 0:16], in0=cur[:, 15:31], in1=cnt_t[:, g * 16:(g + 1) * 16], op=ALU.mult),
                         reads=(curB, cstB), writes=(f1B,))
                    S.op("dve", lambda e, f1=f1, c=c: e.tensor_tensor(out=ycat[:, c, 0:16], in0=f1[:, 0:16], in1=ue[:, c, 15:31], op=ALU.subtract),
                         reads=(f1B, ueB[c]), writes=(ycatB[c],))
                bk, bkB = proj_chunk(sg, lz, wzB, q * 128)
                S.op("act", lambda e, bk=bk, q=q: e.activation(out=szb[:, q, 0:T], in_=bk[:, 0:T], func=AF.Silu), reads=(bkB,), writes=(szbB[q],))
            lpm, pmB = use_block(f"Pm{h}")
            for q in range(4):
                c = h * 4 + q
                g, hh = c // 2, c % 2
                bk, bkB = nbank()
                S.op("pe", lambda e, bk=bk, g=g, hh=hh, h=h, lpm=lpm: mm_group(e, bk[:, 0:T], [(lpm((g - 2 * h) * 2 + k2, hh * 128), ycat[:, 2 * g + k2, 0:T]) for k2 in range(2)]),
                     reads=(pmB, ycatB[2 * g], ycatB[2 * g + 1]), writes=(bkB,))
                S.op("dve", lambda e, bk=bk, c=c, q=q: e.scalar_tensor_tensor(out=sq[:, c, 0:T], in0=bk[:, 0:T], scalar=pscv(c), in1=szb[:, q, 0:T], op0=ALU.mult, op1=ALU.mult),
                     reads=(bkB, szbB[q], parT), writes=(sqB[c],))
        if hook is not None:
            hook()
        out_proj_l1(sg)
        ple(sg, 1)

    def out_proj_l1(sg):
        T = sg.T
        for h in range(2):
            lhs, wB = use_block(f"Woo{h}")
            pm_ = proj_multi(sg, [(lhs, wB, q_ * 128) for q_ in range(4)], src=sq, srcB=sqB) if h == 0 else None
            for q in range(4):
                oc = h * 4 + q
                bk, bkB = pm_[q] if pm_ is not None else proj_chunk(sg, lhs, wB, q * 128, src=sq, srcB=sqB)
                S.op("dve", lambda e, oc=oc, bk=bk: e.tensor_tensor(out=hT[:, oc, 0:T], in0=bk[:, 0:T], in1=hT[:, oc, 0:T], op=ALU.add), reads=(bkB, hTB[oc]), writes=(hTB[oc],))
                S.op("act", lambda e, oc=oc: e.activation(out=hbf[:, oc, 0:T], in_=hT[:, oc, 0:T], func=AF.Copy), reads=(hTB[oc],), writes=(hbfB[oc],))

    hbf32 = hbf.bitcast(F32)[:].rearrange("p a b -> p (a b)").rearrange("p (c t) -> p c t", t=512)
    ycat32 = ycat.bitcast(F32)[:].rearrange("p a b -> p (a b)").rearrange("p (c t) -> p c t", t=512)

    hu0 = xb.bitcast(BF16)[:].rearrange("p a b -> p (a b)").rearrange("p (c t) -> p c t", t=512)
    hu0B = [xbB[c // 2] for c in range(8)]

    def yn_view(c):
        if c < 4:
            return hbf32[:, c, :], (hbfB[2 * c], hbfB[2 * c + 1])
        return ycat32[:, c - 4, :], (ycatB[2 * (c - 4)], ycatB[2 * (c - 4) + 1])

    rtok = sb("rtok", [128, 8]); rtokB = Buf("rtok")

    def final_a(sg):
        T = sg.T
        bk, bkB = nbank()
        subs = []
        for sub in range(sg.nsub):
            for c in range(8):
                subs.append((lambda e, sub=sub, c=c: e.matmul(bk[:, sub:sub + 1], lhsT=sq[:, c, sub * 128:(sub + 1) * 128], rhs=onesb[:, 0:1],
                                                            start=(c == 0), stop=(c == 7)), (sqB[c], constB)))
        S.group("pe", subs, writes=(bkB,))
        S.op("act", lambda e: e.activation(out=rtok[:, 4:4 + sg.nsub], in_=bk[:, 0:sg.nsub], func=AF.Sqrt, bias=EPS, scale=1.0 / 1024.0), reads=(bkB,), writes=(rtokB,))
        S.op("dve", lambda e: e.reciprocal(out=rtok[:, 0:sg.nsub], in_=rtok[:, 4:4 + sg.nsub]), reads=(rtokB,), writes=(rtokB,))
        for c in range(8):
            yv, yB = yn_view(c)
            S.op("dve", lambda e, c=c, yv=yv: e.tensor_scalar(out=yv[:, 0:T], in0=hT[:, c, 0:T], scalar1=g32[:, c, 2:3], scalar2=None, op0=ALU.mult),
                 reads=(hTB[c], parT), writes=yB)

    def final_b(sg):
        for sub in range(sg.nsub):
            io = nio()
            for half in range(2):
                bk, bkB = nbank()

                def _tp(e, bk=bk, half=half, sub=sub):
                    for q in range(4):
                        yv, _ = yn_view(half * 4 + q)
                        i = e.transpose(out=bk[:, q * 128:(q + 1) * 128], in_=yv[:, sub * 128:(sub + 1) * 128], identity=ident)
                    return i
                rb = ()
                for q in range(4):
                    rb = rb + yn_view(half * 4 + q)[1]
                S.op("pe", _tp, reads=rb + (cstB,), writes=(bkB,))
                if half == 0:
                    S.op("act", lambda e, bk=bk, io=io, sub=sub: e.activation(out=xio[io][:, 0:512], in_=bk[:, :], func=AF.Copy, scale=rtok[:, sub:sub + 1]),
                         reads=(bkB, rtokB), writes=(xioB[io],))
                else:
                    S.op("dve", lambda e, bk=bk, io=io, sub=sub: e.tensor_scalar(out=xio[io][:, 512:1024], in0=bk[:, :], scalar1=rtok[:, sub:sub + 1], scalar2=None, op0=ALU.mult),
                         reads=(bkB, rtokB), writes=(xioB[io],))
            if sg.kind == "P":
                r0 = sg.ti * 512 + sub * 128
                dst = y_p[r0:r0 + 128, :]
            else:
                dst = y_s
            iodma(f"xio{io}", lambda e, io=io, dst=dst: e.dma_start(out=dst, in_=xio[io][:]), reads=(xioB[io],))

    def halo_shift(sg):
        S.op("dve", lambda e: e.tensor_copy(out=cxe[:, :, 0:2], in_=cxe[:, :, 512:514]), writes=tuple(cxB))
        S.op("dve", lambda e: e.tensor_copy(out=glu[:, :, 0:30], in_=glu[:, :, 512:542]), writes=tuple(gluB))
        S.op("dve", lambda e: e.tensor_copy(out=ue[:, :, 0:15], in_=ue[:, :, 512:527]), writes=tuple(ueB))

    def state_out_prompt(sg):
        io0 = nio()
        sst, sstB = xio[io0], xioB[io0]
        bk, bkB = nbank()

        def _t1(e, bk=bk):
            for j in range(4):
                i = e.transpose(out=bk[0:32, j * 128:(j + 1) * 128], in_=cxe[:, j, 482:514], identity=ident)
            return i
        S.op("pe", _t1, reads=tuple(cxB) + (cstB,), writes=(bkB,))
        S.op("act", lambda e, bk=bk, sst=sst: e.activation(out=sst[0:32, 0:512], in_=bk[0:32, :], func=AF.Copy), reads=(bkB,), writes=(sstB,))
        S.dma("sp", "stp_a", lambda e, sst=sst: e.dma_start(out=na_p, in_=sst[30:32, 0:512]), reads=(sstB,))
        bk2, bk2B = nbank()

        def _t2(e, bk2=bk2):
            for j in range(4):
                i = e.transpose(out=bk2[0:32, j * 128:(j + 1) * 128], in_=gl32[:, j, 0:32], identity=ident)
            return i
        S.op("pe", _t2, reads=tuple(gl32B) + (cstB,), writes=(bk2B,))
        S.op("act", lambda e, bk2=bk2, sst=sst: e.activation(out=sst[0:32, 512:1024], in_=bk2[0:32, :], func=AF.Copy, scale=0.5), reads=(bk2B,), writes=(sstB,))
        S.dma("sp", "stp_b", lambda e, sst=sst: e.dma_start(out=nb_p, in_=sst[2:32, 512:1024]), reads=(sstB,))
        io = nio()
        for half in range(2):
            bk3, bk3B = nbank()

            def _t3(e, bk3=bk3, half=half):
                for q in range(4):
                    i = e.transpose(out=bk3[0:32, q * 128:(q + 1) * 128], in_=ue[:, half * 4 + q, 495:527], identity=ident)
                return i
            S.op("pe", _t3, reads=tuple(ueB[half * 4:half * 4 + 4]) + (cstB,), writes=(bk3B,))
            S.op("act", lambda e, bk3=bk3, half=half, io=io: e.activation(out=xio[io][0:32, half * 512:(half + 1) * 512], in_=bk3[0:32, :], func=AF.Copy), reads=(bk3B,), writes=(xioB[io],))
        S.dma("sp", "stp_c", lambda e, io=io: e.dma_start(out=np_p, in_=xio[io][17:32, :]), reads=(xioB[io],))

    def state_in_sample():
        stc = {"i": 0}

        def nst():
            i = nio()
            return xio[i], xioB[i], f"xio{i}"
        sst, sstB, skey = nst()
        iodma(skey, lambda e, sst=sst: e.dma_start(out=sst[0:32, 0:512], in_=sta), writes=(sstB,))
        bk, bkB = nbank()

        def _t1(e, bk=bk, sst=sst):
            for j in range(4):
                i = e.transpose(out=bk[:, j * 32:(j + 1) * 32], in_=sst[0:32, j * 128:(j + 1) * 128], identity=ident[0:32, 0:32])
            return i
        S.op("pe", _t1, reads=(sstB, cstB), writes=(bkB,))
        S.op("dve", lambda e, bk=bk: e.tensor_copy(out=cxe[:, :, 0:160].rearrange("p j (s l) -> p j s l", l=10)[:, :, :, 0:2],
                                                   in_=bk[:, 0:128].rearrange("p (j s r) -> p j s r", s=16, r=2)),
             reads=(bkB,), writes=tuple(cxB))
        for rb in range(4):
            sst, sstB, skey = nst()
            iodma(skey, lambda e, rb=rb, sst=sst: e.dma_start(out=sst[0:120, 0:512], in_=stb[rb * 4:(rb + 1) * 4].rearrange("s r f -> (s r) f")), writes=(sstB,))
            bk, bkB = nbank()

            def _t2(e, bk=bk, sst=sst):
                for j in range(4):
                    i = e.transpose(out=bk[:, j * 120:(j + 1) * 120], in_=sst[0:120, j * 128:(j + 1) * 128], identity=ident[0:120, 0:120])
                return i
            S.op("pe", _t2, reads=(sstB, cstB), writes=(bkB,))
            S.op("dve", lambda e, bk=bk, rb=rb: e.tensor_scalar(
                out=glu[:, :, 0:608].rearrange("p j (s l) -> p j s l", l=38)[:, :, rb * 4:(rb + 1) * 4, 0:30],
                in0=bk[:, 0:480].rearrange("p (j s r) -> p j s r", s=4, r=30), scalar1=2.0, scalar2=None, op0=ALU.mult),
                reads=(bkB,), writes=tuple(gluB))
        for rb in range(2):
            sst, sstB, skey = nst()
            iodma(skey, lambda e, rb=rb, sst=sst: e.dma_start(out=sst[0:120, :], in_=stp[rb * 8:(rb + 1) * 8].rearrange("s r f -> (s r) f")), writes=(sstB,))
            for half in range(2):
                bk, bkB = nbank()

                def _t3(e, bk=bk, half=half, sst=sst):
                    for q in range(4):
                        c = half * 4 + q
                        i = e.transpose(out=bk[:, q * 120:(q + 1) * 120], in_=sst[0:120, c * 128:(c + 1) * 128], identity=ident[0:120, 0:120])
                    return i
                S.op("pe", _t3, reads=(sstB, cstB), writes=(bkB,))
                S.op("dve", lambda e, bk=bk, rb=rb, half=half: e.tensor_copy(
                    out=ue[:, half * 4:half * 4 + 4, 0:368].rearrange("p c (s l) -> p c s l", l=23)[:, :, rb * 8:(rb + 1) * 8, 0:15],
                    in_=bk[:, 0:480].rearrange("p (c s r) -> p c s r", s=8, r=15)),
                    reads=(bkB,), writes=tuple(ueB[half * 4:half * 4 + 4]))

    def state_out_sample(sg):
        iodma("st_out", lambda e: e.dma_start(out=nb_s[:, 0:22, :], in_=stb[:, 8:30, :]))
        iodma("st_out", lambda e: e.dma_start(out=np_s[:, 0:7, :], in_=stp[:, 8:15, :]))
        c1, c1B = ntmp()
        S.op("dve", lambda e: e.tensor_copy(out=c1[:, 0:128].rearrange("p (j s r) -> p j s r", s=16, r=2),
                                            in_=cxe[:, :, 0:160].rearrange("p j (s l) -> p j s l", l=10)[:, :, :, 8:10]),
             reads=tuple(cxB), writes=(c1B,))
        bk, bkB = nbank()

        def _t1(e, bk=bk):
            for j in range(4):
                i = e.transpose(out=bk[0:32, j * 128:(j + 1) * 128], in_=c1[:, j * 32:(j + 1) * 32], identity=ident)
            return i
        S.op("pe", _t1, reads=(c1B, cstB), writes=(bkB,))
        io = nio()
        S.op("act", lambda e, bk=bk, io=io: e.activation(out=xio[io][0:32, 0:512], in_=bk[0:32, :], func=AF.Copy), reads=(bkB,), writes=(xioB[io],))
        iodma(f"xio{io}", lambda e, io=io: e.dma_start(out=na_s, in_=xio[io][0:32, 0:512]), reads=(xioB[io],))
        bk, bkB = nbank()

        def _t2(e, bk=bk):
            for j in range(4):
                i = e.transpose(out=bk[:, j * 128:(j + 1) * 128], in_=gl32[:, j, 0:128], identity=ident)
            return i
        S.op("pe", _t2, reads=tuple(gl32B) + (cstB,), writes=(bkB,))
        io = nio()
        S.op("act", lambda e, bk=bk, io=io: e.activation(out=xio[io][:, 0:512], in_=bk[:, :], func=AF.Copy, scale=0.5), reads=(bkB,), writes=(xioB[io],))
        for s in range(16):
            iodma(f"xio{io}", lambda e, io=io, s=s: e.dma_start(out=nb_s[s, 22:30, :], in_=xio[io][s * 8:(s + 1) * 8, 0:512]), reads=(xioB[io],))
        io = nio()
        for half in range(2):
            c2, c2B = ntmp()
            S.op("dve", lambda e, c2=c2, half=half: e.tensor_copy(
                out=c2[:, 0:512].rearrange("p (c s t) -> p c s t", s=16, t=8),
                in_=ue[:, half * 4:half * 4 + 4, 0:368].rearrange("p c (s l) -> p c s l", l=23)[:, :, :, 15:23]),
                reads=tuple(ueB[half * 4:half * 4 + 4]), writes=(c2B,))
            bk, bkB = nbank()

            def _t3(e, bk=bk, c2=c2):
                for q in range(4):
                    i = e.transpose(out=bk[:, q * 128:(q + 1) * 128], in_=c2[:, q * 128:(q + 1) * 128], identity=ident)
                return i
            S.op("pe", _t3, reads=(c2B, cstB), writes=(bkB,))
            S.op("act", lambda e, bk=bk, io=io, half=half: e.activation(out=xio[io][:, half * 512:(half + 1) * 512], in_=bk[:, :], func=AF.Copy), reads=(bkB,), writes=(xioB[io],))
        for s in range(16):
            iodma(f"xio{io}", lambda e, io=io, s=s: e.dma_start(out=np_s[s, 7:15, :], in_=xio[io][s * 8:(s + 1) * 8, :]), reads=(xioB[io],))

    segs = [Seg("P", t) for t in range(NPT)] + [Seg("S", NPT)]

    def enter(sg):
        S.ctx = f"{sg.kind}{sg.ti}"
        st_["ioq"] = "sp" if sg.ti == 0 else "pool"
    enter(segs[0])
    load_tile(segs[0])
    make_hu0(segs[0])
    segs[0].rr0 = rms_norm_in(segs[0], 0)
    for si_, sg in enumerate(segs):
        enter(sg)
        lastP = (sg.kind == "P" and sg.ti == NPT - 1)
        layer0(sg, lastP or sg.kind == "S")
        if DBG in ("ycat", "mixer", "L0"):
            break
        layer1(sg, sg.kind == "P" and sg.ti == 0, hook=(lambda sg=sg: state_out_prompt(sg)) if lastP else None)
        if DBG == "L1":
            break
        if sg.kind == "S":
            state_out_sample(sg)
        if sg.kind == "P" and not lastP:
            halo_shift(sg)
        final_a(sg)
        if si_ + 1 < len(segs):
            nx = segs[si_ + 1]
            enter(nx)
            if nx.kind == "S":
                state_in_sample()
            load_tile(nx)
            make_hu0(nx)
            nx.rr0 = rms_norm_in(nx, 0)
            enter(sg)
        final_b(sg)
    S.final_wait("sp")

    sems = {k: es.enter_context(nc.semaphore(f"sem_{k}")) for k in S.sem_keys()}
    block = es.enter_context(nc.Block())

    @block.tensor
    def _(e):
        S.replay("pe", e, sems)

    @block.scalar
    def _(e):
        S.replay("act", e, sems)

    @block.vector
    def _(e):
        S.replay("dve", e, sems)

    @block.gpsimd
    def _(e):
        S.replay("pool", e, sems)

    @block.sync
    def _(e):
        S.replay("sp", e, sems)

    es.close()
    nc._sched = S
    return nc


def make_consts():
    c = np.zeros((128, 192), np.float32)
    c[:, 0:128] = np.eye(128, dtype=np.float32)
    for g, w in enumerate((2, 4, 8, 16)):
        c[:, 128 + g * 16:128 + (g + 1) * 16] = (1.0 / np.minimum(np.arange(16) + 1, w)).astype(np.float32)[None, :]
    return c


def core_inputs(i, NPT, x_prompt, x_sample, state_conv_a, state_conv_b, state_pool, p_prompt, p_sample,
                norm_g, w_in_even, conv_a_w, conv_b_w, conv_b_bias, ln_b_gamma, ln_b_beta, w_out_even,
                w_in_odd, pool_map, pool_scale, w_out_odd, ple_proj, ple_gate, final_norm_g):
    f = lambda a: np.ascontiguousarray(a, dtype=np.float32)
    s0 = i * 16
    PA = np.concatenate([conv_a_w[0], conv_b_w[0], conv_b_bias, ln_b_gamma, ln_b_beta], axis=0)
    PB = np.concatenate([norm_g, pool_scale, final_norm_g[None, :]], axis=0)
    return {
        "xp": f(x_prompt[i]), "xs": f(x_sample[s0:s0 + 16].reshape(128, 1024)),
        "sta": f(state_conv_a[0, s0:s0 + 16].reshape(32, 512)), "stb": f(state_conv_b[0, s0:s0 + 16]),
        "stp": f(state_pool[0, s0:s0 + 16]),
        "pp": f(p_prompt[:, i]), "psm": f(p_sample[:, s0:s0 + 16].reshape(2, 128, 256)),
        "PA": f(PA), "PB": f(PB),
        "w_ie": f(w_in_even[0]), "w_oe": f(w_out_even[0]), "w_io": f(w_in_odd[0]),
        "pmap": f(pool_map[0].reshape(1024, 256)), "w_oo": f(w_out_odd[0]),
        "plp": f(ple_proj.reshape(512, 1024)), "plg": f(ple_gate.reshape(2048, 1024)),
        "cst": make_consts(),
    }


_NC_CACHE = {}


def kernel(**inputs):
    inputs = {k: np.asarray(v) for k, v in inputs.items()}
    NPT = inputs["x_prompt"].shape[1] // 512
    if NPT not in _NC_CACHE:
        _NC_CACHE[NPT] = build(NPT)
    nc = _NC_CACHE[NPT]
    in_maps = [core_inputs(i, NPT, **inputs) for i in range(NCORES)]
    res = run_bass_kernel_spmd(nc, in_maps, core_ids=list(range(NCORES)))
    R = res.results
    SEQ = NPT * 512
    y_prompt = np.stack([R[i]["y_p"] for i in range(NCORES)], 0).reshape(NCORES, SEQ, 1024)
    y_sample = np.concatenate([R[i]["y_s"].reshape(16, 8, 1024) for i in range(NCORES)], 0)
    na_p = np.stack([R[i]["na_p"] for i in range(NCORES)], 0)[None]
    nb_p = np.stack([R[i]["nb_p"] for i in range(NCORES)], 0)[None]
    np_p = np.stack([R[i]["np_p"] for i in range(NCORES)], 0)[None]
    na_s = np.concatenate([R[i]["na_s"].reshape(16, 2, 512) for i in range(NCORES)], 0)[None]
    nb_s = np.concatenate([R[i]["nb_s"] for i in range(NCORES)], 0)[None]
    np_s = np.concatenate([R[i]["np_s"] for i in range(NCORES)], 0)[None]
    f = lambda a: np.ascontiguousarray(a, dtype=np.float32)
    return tuple(f(a) for a in (y_prompt, y_sample, na_p, nb_p, np_p, na_s, nb_s, np_s))
```

```python
import contextlib
import sys
import numpy as np
import concourse.bass as bass
import concourse.mybir as mybir
from concourse.bass_utils import run_bass_kernel_spmd

F32, BF16 = mybir.dt.float32, mybir.dt.bfloat16
ALU = mybir.AluOpType
AF = mybir.ActivationFunctionType
EPS = 1e-6
NCORES = 8
NSLOT = 5
NTMP = 9
TW = 528


class Buf:
    __slots__ = ("name", "w", "r", "excl")

    def __init__(self, name, excl=False):
        self.name, self.w, self.r, self.excl = name, None, {}, excl


class Sched:
    ENGS = ("pe", "act", "dve", "pool", "sp")

    def __init__(self):
        self.ops = {e: [] for e in self.ENGS}
        self.cnt = {e: 0 for e in self.ENGS}
        self.waited = {e: {} for e in self.ENGS}
        self.dcnt = {}
        self.labels = {e: [] for e in self.ENGS}
        self.ctx = "setup"

    def _deps(self, eng, reads, writes):
        need = {}

        def add(ev):
            if ev is not None and need.get(ev[0], 0) < ev[1]:
                need[ev[0]] = ev[1]
        for b in reads:
            add(b.w)
            if b.excl:
                for k, v in b.r.items():
                    if k != eng:
                        add((k, v))
        for b in writes:
            add(b.w)
            for k, v in b.r.items():
                add((k, v))
        waits = []
        for k, v in need.items():
            if self.waited[eng].get(k, 0) >= v:
                continue
            self.waited[eng][k] = v
            waits.append((k, v))
        return waits

    def _mark(self, ev, reads, writes):
        for b in reads:
            if b.r.get(ev[0], 0) < ev[1]:
                b.r[ev[0]] = ev[1]
        for b in writes:
            b.w, b.r = ev, {}

    def op(self, eng, fn, reads=(), writes=()):
        waits = self._deps(eng, reads, writes)
        self.cnt[eng] += 1
        ev = (eng, self.cnt[eng])
        self.ops[eng].append((waits, fn, ev, 1))
        self.labels[eng].append(self.ctx + ":" + sys._getframe(1).f_code.co_name)
        self._mark(ev, reads, writes)

    def group(self, eng, subs, writes=()):
        self.cnt[eng] += 1
        ev = (eng, self.cnt[eng])
        lab = self.ctx + ":" + sys._getframe(1).f_code.co_name
        n = len(subs)
        for i, (fn, reads) in enumerate(subs):
            waits = self._deps(eng, reads, writes if i == 0 else ())
            last = (i == n - 1)
            self.ops[eng].append((waits, fn, ev if last else None, 1 if last else 0))
            self.labels[eng].append(lab)
            for b in reads:
                if b.r.get(ev[0], 0) < ev[1]:
                    b.r[ev[0]] = ev[1]
        for b in writes:
            b.w, b.r = ev, {}

    def dma(self, eng, key, fn, reads=(), writes=()):
        waits = self._deps(eng, reads, writes)
        self.dcnt[key] = self.dcnt.get(key, 0) + 16
        ev = (key, self.dcnt[key])
        self.ops[eng].append((waits, fn, ev, 16))
        self.labels[eng].append(self.ctx + ":dma:" + sys._getframe(1).f_code.co_name)
        self._mark(ev, reads, writes)

    def sem_keys(self):
        return list(self.ENGS) + list(self.dcnt.keys())

    def final_wait(self, eng):
        waits = []
        for k in self.ENGS:
            if k != eng and self.cnt[k] > self.waited[eng].get(k, 0):
                waits.append((k, self.cnt[k]))
        for k, v in self.dcnt.items():
            if v > self.waited[eng].get(k, 0):
                waits.append((k, v))
        self.ops[eng].append((waits, None, None, 0))

    def replay(self, eng, e, sems):
        for waits, fn, ev, inc in self.ops[eng]:
            for k, v in waits:
                e.wait_ge(sems[k], v)
            if fn is None:
                continue
            ins = fn(e)
            if ev is not None:
                ins.then_inc(sems[ev[0]], inc)


DBG = None


def build(NPT):
    SEQ = NPT * 512
    nc = bass.Bass("TRN2", target_bir_lowering=False)

    def din(name, shape):
        return nc.dram_tensor(name, list(shape), F32, kind="ExternalInput").ap()

    def dout(name, shape):
        return nc.dram_tensor(name, list(shape), F32, kind="ExternalOutput").ap()

    xp = din("xp", [SEQ, 1024]); xs = din("xs", [128, 1024])
    sta = din("sta", [32, 512]); stb = din("stb", [16, 30, 512]); stp = din("stp", [16, 15, 1024])
    pp = din("pp", [2, SEQ, 256]); psm = din("psm", [2, 128, 256])
    PA = din("PA", [37, 512]); PB = din("PB", [4, 1024])
    w_ie = din("w_ie", [1024, 3584]); w_oe = din("w_oe", [1024, 1024])
    w_io = din("w_io", [1024, 2048]); pmap = din("pmap", [1024, 256]); w_oo = din("w_oo", [1024, 1024])
    plp = din("plp", [512, 1024]); plg = din("plg", [2048, 1024])
    cst = din("cst", [128, 128 + 64])
    y_p = dout("y_p", [SEQ, 1024]); y_s = dout("y_s", [128, 1024])
    na_p = dout("na_p", [2, 512]); nb_p = dout("nb_p", [30, 512]); np_p = dout("np_p", [15, 1024])
    na_s = dout("na_s", [32, 512]); nb_s = dout("nb_s", [16, 30, 512]); np_s = dout("np_s", [16, 15, 1024])

    S = Sched()
    es = contextlib.ExitStack()

    def sb(name, shape, dt=F32):
        return es.enter_context(nc.sbuf_tensor(name, list(shape), dt))

    ring = [sb(f"ring{i}", [128, 8 * 512], BF16) for i in range(NSLOT)]
    ringB = [Buf(f"ring{i}") for i in range(NSLOT)]
    hT = sb("hT", [128, 8, 512]); hTB = [Buf(f"hT{c}") for c in range(8)]
    sq = sb("sq", [128, 8, 512], BF16); sqB = [Buf(f"sq{c}") for c in range(8)]
    hbf = sb("hbf", [128, 8, 512], BF16); hbfB = [Buf(f"hbf{c}") for c in range(8)]
    ycat = sb("ycat", [128, 8, 512], BF16); ycatB = [Buf(f"ycat{c}") for c in range(8)]
    NIO = 3
    xio = [sb(f"xio{i}", [128, 1024]) for i in range(NIO)]; xioB = [Buf(f"xio{i}") for i in range(NIO)]
    pin = [sb(f"pin{i}", [128, 2, 256]) for i in range(2)]; pinB = [Buf(f"pin{i}") for i in range(2)]
    pT = sb("pT", [128, 4, 512], BF16); pTB = [Buf(f"pT{l}") for l in range(2)]
    glu = sb("glu", [128, 4, 608], BF16); gluB = [Buf(f"glu{j}") for j in range(4)]
    cxe = sb("cxe", [128, 4, 514]); cxB = [Buf(f"cx{j}") for j in range(4)]
    ue = sb("ue", [128, 8, 528]); ueB = [Buf(f"ue{c}") for c in range(8)]
    diag = sb("diag", [128, 4, 31, 128], BF16); diagB = Buf("diag")
    xb = sb("xb", [128, 4, 512]); xbB = [Buf(f"xb{j}") for j in range(4)]
    szb = sb("szb", [128, 4, 512], BF16); szbB = [Buf(f"szb{j}") for j in range(4)]
    xbb = sb("xbb", [128, 4, 512], BF16); xbbB = [Buf(f"xbb{j}") for j in range(4)]
    sqb = sb("sqb", [128, 4, 512], BF16); sqbB = [Buf(f"sqb{j}") for j in range(4)]
    gl32 = sb("gl32", [128, 4, 128]); gl32B = [Buf(f"gl32{j}") for j in range(4)]
    lnm = sb("lnm", [128, 512]); lnmB = Buf("lnm"); lnr = sb("lnr", [128, 512]); lnrB = Buf("lnr")
    tmp = [sb(f"tmp{i}", [128, TW]) for i in range(NTMP)]; tmpB = [Buf(f"tmp{i}") for i in range(NTMP)]
    cst_sb = sb("cst_sb", [128, 192]); cstB = Buf("cst")
    PAs = xio[2][0:37, 0:512]; PBs = xio[1][0:4, :]; parB = Buf("parstage")
    pa = sb("pa", [128, 4, 37]); pb = sb("pb", [128, 8, 4]); g32 = sb("g32", [128, 8, 3]); parT = Buf("parT")
    wbh = sb("wbh", [128, 4, 31])
    identb = sb("identb", [128, 128], BF16); onesb = sb("onesb", [128, 128], BF16)
    c512 = sb("c512", [128, 128], BF16); constB = Buf("constb")
    sst = xio[2]; sstB = xioB[2]
    ident = cst_sb[:, 0:128]
    cnt_t = cst_sb[:, 128:192]

    banks = [es.enter_context(nc.psum_tensor(f"bank{i}", [128, 512], F32)) for i in range(8)]
    bankB = [Buf(f"bank{i}", excl=True) for i in range(8)]
    st_ = {"bank": 0, "tmp": 0, "io": 0, "pin": 0, "ioq": "sp"}

    def iodma(key, fn, reads=(), writes=()):
        q = st_["ioq"]
        S.dma(q, f"{key}_{q}", fn, reads=reads, writes=writes)

    def nbank():
        i = st_["bank"]; st_["bank"] = (i + 1) % 8
        return banks[i], bankB[i]

    def ntmp():
        i = st_["tmp"]; st_["tmp"] = (i + 1) % NTMP
        return tmp[i], tmpB[i]

    def nio():
        i = st_["io"]; st_["io"] = (i + 1) % 3
        return i

    def wview(w, ncols_total):
        return w.rearrange("(kc p) n -> p kc n", p=128)

    v_ie = w_ie.rearrange("(kc p) (s c) -> p kc s c", p=128, c=512)
    blocks = []

    def blk_list():
        L = []
        for nm, sp in (("Sv", 4), ("Sg", 5), ("Sz", 6), ("Sbg", 0), ("Scg", 1), ("Sx", 2), ("Saz", 3)):
            L.append((nm, wview(w_ie, 3584)[:, :, sp * 512:(sp + 1) * 512], 8, (512,)))
        for h in range(2):
            L.append((f"Woe{h}", wview(w_oe, 1024)[:, :, h * 512:(h + 1) * 512], 8, (512,)))
        def wp(l, h):
            return (f"Wp{l}{h}", plp[l * 256:(l + 1) * 256, :].rearrange("(kc p) n -> p kc n", p=128)[:, :, h * 512:(h + 1) * 512], 2, (512,))

        def wg(l, h):
            return (f"Wg{l}{h}", plg[l * 1024:(l + 1) * 1024, :].rearrange("(kc p) n -> p kc n", p=128)[:, :, h * 512:(h + 1) * 512], 8, (512,))
        for h in range(2):
            L.append(wp(0, h)); L.append(wg(0, h))
        pmv = pmap.rearrange("(gk p) n -> p gk n", p=128)
        for h in range(2):
            L.append((f"Bu{h}", wview(w_io, 2048)[:, :, h * 512:(h + 1) * 512], 8, (512,)))
            L.append((f"Bz{h}", wview(w_io, 2048)[:, :, 1024 + h * 512:1024 + (h + 1) * 512], 8, (512,)))
            L.append((f"Pm{h}", pmv[:, h * 4:(h + 1) * 4, :], 4, (256,)))
        for h in range(2):
            L.append((f"Woo{h}", wview(w_oo, 1024)[:, :, h * 512:(h + 1) * 512], 8, (512,)))
        for h in range(2):
            L.append(wp(1, h)); L.append(wg(1, h))
        return L

    NT = NPT + 1
    allblk = []
    for t in range(NT):
        for b in blk_list():
            allblk.append(b)
    wst = {"issued": 0, "use": 0}

    def slot_view(si, kc, fshape):
        n = int(np.prod(fshape))
        v = ring[si][:, 0:kc * n].rearrange("p (k n) -> p k n", n=n)
        if len(fshape) == 2:
            v = v.rearrange("p k (s c) -> p k s c", c=fshape[1])
        return v

    NB = len(blk_list())
    wscr = nc.dram_tensor("wscr", [NB, 128, 4096], BF16).ap()
    scrB = [Buf(f"scr{b}") for b in range(NB)]

    def issue_weights(upto):
        while wst["issued"] < min(upto, len(allblk)):
            bi = wst["issued"]; si = bi % NSLOT
            pass_, b = bi // NB, bi % NB
            name, src, kc, fshape = allblk[bi]
            n = int(np.prod(fshape))
            used = kc * n
            defer = (len(fshape) == 1 and kc == 8 and (b % 2 == 0))
            if pass_ == 0 or (pass_ == 1 and defer):
                dv = ring[si][:, 0:used].rearrange("p (k n) -> p k n", n=n)
                if len(fshape) == 2:
                    pairs = [(dv[:, :, s_ * fshape[1]:(s_ + 1) * fshape[1]], src[:, :, s_, :]) for s_ in range(fshape[0])]
                else:
                    pairs = [(dv, src)]
                for dst, sr in pairs:
                    S.dma("pool", f"ring{si}", lambda e, dst=dst, sr=sr: e.dma_start(out=dst, in_=sr),
                          reads=(), writes=(ringB[si],))
                if (pass_ == 0) != defer:
                    S.dma("sp", f"wb{si}", lambda e, b=b, si=si, used=used: e.dma_start(out=wscr[b, :, 0:used], in_=ring[si][:, 0:used]),
                          reads=(ringB[si],), writes=(scrB[b],))
            else:
                S.dma("sp", f"ringh{si}", lambda e, b=b, si=si, used=used: e.dma_start(out=ring[si][:, 0:used], in_=wscr[b, :, 0:used]),
                      reads=(scrB[b],), writes=(ringB[si],))
            wst["issued"] += 1

    def use_group(names):
        bi0 = wst["use"]
        issue_weights(bi0 + NSLOT)
        out = []
        for i, expect in enumerate(names):
            bi = bi0 + i; si = bi % NSLOT
            name, src, kc, fshape = allblk[bi]
            assert name == expect, (name, expect)
            n = int(np.prod(fshape))
            v = ring[si][:, 0:kc * n].rearrange("p (k n) -> p k n", n=n)
            out.append(((lambda k, c0, v=v: v[:, k, c0:c0 + 128]), ringB[si]))
        wst["use"] += len(names)
        return out

    def use_block(expect):
        return use_group([expect])[0]

    S.dma("sp", "ld_cst", lambda e: e.dma_start(out=cst_sb[:], in_=cst), writes=(cstB,))
    S.dma("sp", "ld_par", lambda e: e.dma_start(out=PAs, in_=PA), writes=(parB, xioB[2]))
    S.dma("sp", "ld_par", lambda e: e.dma_start(out=PBs, in_=PB), writes=(parB, xioB[1]))
    S.op("dve", lambda e: e.tensor_copy(out=identb[:], in_=ident), reads=(cstB,), writes=(constB,))
    S.op("dve", lambda e: e.memset(onesb[:], 1.0), writes=(constB,))
    S.op("dve", lambda e: e.memset(c512[:], 1.0 / 512.0), writes=(constB,))
    S.op("dve", lambda e: e.memset(cxe[:, :, 0:2], 0.0), writes=tuple(cxB))
    S.op("dve", lambda e: e.memset(glu[:, :, 0:30], 0.0), writes=tuple(gluB))
    S.op("dve", lambda e: e.memset(ue[:, :, 0:15], 0.0), writes=tuple(ueB))
    bk, bkB = nbank()

    def _tp_pa(e):
        for j in range(4):
            i = e.transpose(out=bk[:, j * 37:(j + 1) * 37], in_=PAs[:, j * 128:(j + 1) * 128], identity=ident[0:37, 0:37])
        for c in range(8):
            i = e.transpose(out=bk[:, 160 + c * 4:160 + (c + 1) * 4], in_=PBs[:, c * 128:(c + 1) * 128], identity=ident[0:4, 0:4])
        return i
    S.op("pe", _tp_pa, reads=(parB, cstB, xioB[1], xioB[2]), writes=(bkB,))
    S.op("dve", lambda e: e.tensor_copy(out=pa[:].rearrange("p j r -> p (j r)"), in_=bk[:, 0:148]), reads=(bkB,), writes=(parT,))
    S.op("dve", lambda e: e.tensor_copy(out=pb[:].rearrange("p c r -> p (c r)"), in_=bk[:, 160:192]), reads=(bkB,), writes=(parT,))
    S.op("dve", lambda e: e.tensor_scalar(out=g32[:, :, 0:2], in0=pb[:, :, 0:2], scalar1=1.0, scalar2=None, op0=ALU.mult), reads=(parT,), writes=(parT,))
    S.op("dve", lambda e: e.tensor_scalar(out=g32[:, :, 2:3], in0=pb[:, :, 3:4], scalar1=1.0, scalar2=None, op0=ALU.mult), reads=(parT,), writes=(parT,))
    S.op("dve", lambda e: e.tensor_scalar(out=wbh[:], in0=pa[:, :, 3:34], scalar1=0.5, scalar2=None, op0=ALU.mult), reads=(parT,), writes=(parT,))
    diagBj = [Buf(f"diag{j}") for j in range(4)]

    def build_diag(j):
        S.op("pool", lambda e, j=j: e.tensor_tensor(out=diag[:, j, :, :], in0=identb[:].unsqueeze(1).broadcast_to([128, 31, 128]),
                                                    in1=wbh[:, j, :].unsqueeze(2).broadcast_to([128, 31, 128]), op=ALU.mult),
             reads=(parT, constB), writes=(diagBj[j],))

    cawv = lambda j, k: pa[:, j, k:k + 1]
    biasv = lambda j: pa[:, j, 34:35]
    gamv = lambda j: pa[:, j, 35:36]
    betv = lambda j: pa[:, j, 36:37]
    pscv = lambda c: pb[:, c, 2:3]

    class Seg:
        def __init__(self, kind, ti):
            self.kind, self.ti = kind, ti
            self.T = 512 if kind == "P" else 128
            self.nsub = self.T // 128

        def v(self, ap):
            return ap if self.kind == "P" else ap.rearrange("p (s t) -> p s t", t=8)

        def tap(self, ext, c, H, k):
            if self.kind == "P":
                return ext[:, c, k:k + 512]
            L = H + 8
            return ext[:, c, 0:16 * L].rearrange("p (s l) -> p s l", l=L)[:, :, k:k + 8]

        def new(self, ext, c, H):
            return self.tap(ext, c, H, H)

    def mm_group(e, out_ap, pairs):
        n = len(pairs)
        for i, (l, r) in enumerate(pairs):
            ins = e.matmul(out_ap, lhsT=l, rhs=r, start=(i == 0), stop=(i == n - 1))
        return ins

    def load_tile(sg):
        T = sg.T
        xbst = xb[:].rearrange("p a b -> p (a b)")[:, 0:1024]
        for sub in range(sg.nsub):
            if sg.kind == "P":
                r0 = sg.ti * 512 + sub * 128
                xsrc = xp[r0:r0 + 128, :]; psrc = pp[:, r0:r0 + 128, :].rearrange("l t d -> t l d")
            else:
                xsrc = xs; psrc = psm.rearrange("l t d -> t l d")
            if sg.kind == "P" and sub == 3:
                stg, stgB = xbst, (xbB[0], xbB[1])
                iodma("xbst", lambda e, xsrc=xsrc: e.dma_start(out=xbst, in_=xsrc), writes=stgB)
            else:
                io = nio()
                stg, stgB = xio[io], (xioB[io],)
                iodma(f"xio{io}", lambda e, io=io, xsrc=xsrc: e.dma_start(out=xio[io][:], in_=xsrc), writes=stgB)
            pi = st_["pin"]; st_["pin"] = (pi + 1) % 2
            iodma(f"pin{pi}", lambda e, pi=pi, psrc=psrc: e.dma_start(out=pin[pi][:], in_=psrc), writes=(pinB[pi],))
            for half in range(2):
                bk, bkB = nbank()

                def _tp(e, bk=bk, stg=stg, half=half):
                    for q in range(4):
                        fc = half * 4 + q
                        i = e.transpose(out=bk[:, q * 128:(q + 1) * 128], in_=stg[:, fc * 128:(fc + 1) * 128], identity=ident)
                    return i
                S.op("pe", _tp, reads=stgB + (cstB,), writes=(bkB,))
                src3 = bk[:, :].rearrange("p (q t) -> p q t", t=128)
                S.op("dve", lambda e, half=half, sub=sub, src3=src3: e.tensor_copy(out=hT[:, half * 4:half * 4 + 4, sub * 128:(sub + 1) * 128], in_=src3),
                     reads=(bkB,), writes=tuple(hTB[half * 4:half * 4 + 4]))
                S.op("act", lambda e, half=half, sub=sub, src3=src3: e.activation(out=sq[:, half * 4:half * 4 + 4, sub * 128:(sub + 1) * 128], in_=src3, func=AF.Square),
                     reads=(bkB,), writes=tuple(sqB[half * 4:half * 4 + 4]))
            bk, bkB = nbank()

            def _tpp(e, bk=bk, pi=pi):
                for l in range(2):
                    for kc in range(2):
                        q = l * 2 + kc
                        i = e.transpose(out=bk[:, q * 128:(q + 1) * 128], in_=pin[pi][:, l, kc * 128:(kc + 1) * 128], identity=ident)
                return i
            S.op("pe", _tpp, reads=(pinB[pi], cstB), writes=(bkB,))
            S.op("act", lambda e, bk=bk, sub=sub: e.activation(out=pT[:, :, sub * 128:(sub + 1) * 128],
                                                               in_=bk[:, :].rearrange("p (q t) -> p q t", t=128), func=AF.Copy),
                 reads=(bkB,), writes=tuple(pTB))

    def make_hu0(sg):
        T = sg.T
        for c in range(8):
            S.op("act", lambda e, c=c: e.activation(out=hu0[:, c, 0:T], in_=hT[:, c, 0:T], func=AF.Copy, scale=g32[:, c, 0:1]),
                 reads=(hTB[c], parT), writes=(hu0B[c],))

    def rms_norm_in(sg, gi):
        T = sg.T
        bk, bkB = nbank()
        S.group("pe", [(lambda e, c=c: e.matmul(bk[:, 0:T], lhsT=onesb[:], rhs=sq[:, c, 0:T], start=(c == 0), stop=(c == 7)),
                        (sqB[c], constB)) for c in range(8)], writes=(bkB,))
        t1, t1B = ntmp()
        S.op("act", lambda e: e.activation(out=t1[:, 0:T], in_=bk[:, 0:T], func=AF.Sqrt, bias=EPS, scale=1.0 / 1024.0), reads=(bkB,), writes=(t1B,))
        rr, rrB = ntmp()
        S.op("dve", lambda e: e.reciprocal(out=rr[:, 0:T], in_=t1[:, 0:T]), reads=(t1B,), writes=(rrB,))
        return rr, rrB

    def layer_in(sg, gi):
        T = sg.T
        rr, rrB = rms_norm_in(sg, gi)
        for c in range(8):
            S.op("dve", lambda e, c=c: e.scalar_tensor_tensor(out=hbf[:, c, 0:T], in0=hT[:, c, 0:T], scalar=g32[:, c, gi:gi + 1], in1=rr[:, 0:T],
                                                            op0=ALU.mult, op1=ALU.mult),
                 reads=(hTB[c], rrB, parT), writes=(hbfB[c],))
        return rr, rrB

    def proj_chunk(sg, lhs, wB, col0, src=None, srcB=None, kcs=range(8)):
        T = sg.T
        src = hbf if src is None else src
        srcB = hbfB if srcB is None else srcB
        bk, bkB = nbank()
        kl = list(kcs)
        n = len(kl)
        S.group("pe", [(lambda e, i=i, k=k: e.matmul(bk[:, 0:T], lhsT=lhs(k, col0), rhs=src[:, k, 0:T], start=(i == 0), stop=(i == n - 1)),
                        (wB, srcB[k])) for i, k in enumerate(kl)], writes=(bkB,))
        return bk, bkB

    def proj_multi(sg, specs, src=None, srcB=None, korder=range(8)):
        T = sg.T
        src = hbf if src is None else src
        srcB = hbfB if srcB is None else srcB
        bks = [nbank() for _ in specs]
        subs = []
        kl = list(korder)
        for i, k in enumerate(kl):
            for (bk, bkB), (lhs, wB, c0) in zip(bks, specs):
                subs.append((lambda e, bk=bk, c0=c0, k=k, lhs=lhs, i=i: e.matmul(bk[:, 0:T], lhsT=lhs(k, c0), rhs=src[:, k, 0:T], start=(i == 0), stop=(i == len(kl) - 1)),
                             (wB, srcB[k])))
        S.group("pe", subs, writes=tuple(b for _, b in bks))
        return bks

    def mixerB_front(sg, j, last, grp, rrp=None):
        T = sg.T
        (lv, wvB), (lg_, wgB_), (lz_, wzB_) = grp
        if sg.ti == 0 and sg.kind == "P" and j == 0:
            for jj in range(4):
                build_diag(jj)
        if j in (0, 1):
            (bv, bvB), (bg, bgB), (bz, bzB) = proj_multi(sg, [(lv, wvB, j * 128), (lg_, wgB_, j * 128), (lz_, wzB_, j * 128)], src=hu0, srcB=hu0B)
            rr, rrB = rrp
            outs = []
            for (pb_, pbB) in ((bg, bgB), (bv, bvB), (bz, bzB)):
                t_, tB_ = ntmp()
                S.op("dve", lambda e, t_=t_, pb_=pb_: e.tensor_tensor(out=t_[:, 0:T], in0=pb_[:, 0:T], in1=rr[:, 0:T], op=ALU.mult), reads=(pbB, rrB), writes=(tB_,))
                outs.append((t_, tB_))
            (bg, bgB), (bv, bvB), (bz, bzB) = outs
        else:
            bv, bvB = proj_chunk(sg, lv, wvB, j * 128)
            bg, bgB = proj_chunk(sg, lg_, wgB_, j * 128)
            bz, bzB = proj_chunk(sg, lz_, wzB_, j * 128)
        th, thB = ntmp()
        S.op("act", lambda e: e.activation(out=th[:, 0:T], in_=bg[:, 0:T], func=AF.Tanh, scale=0.5), reads=(bgB,), writes=(thB,))
        S.op("dve", lambda e: e.scalar_tensor_tensor(out=sg.new(glu, j, 30), in0=sg.v(th[:, 0:T]), scalar=1.0, in1=sg.v(bv[:, 0:T]), op0=ALU.add, op1=ALU.mult),
             reads=(thB, bvB), writes=(gluB[j],))
        if last:
            n = 32 if sg.kind == "P" else 128
            S.op("dve", lambda e: e.scalar_tensor_tensor(out=gl32[:, j, 0:n], in0=th[:, T - n:T], scalar=1.0, in1=bv[:, T - n:T], op0=ALU.add, op1=ALU.mult),
                 reads=(thB, bvB), writes=(gl32B[j],))
        S.op("act", lambda e: e.activation(out=szb[:, j, 0:T], in_=bz[:, 0:T], func=AF.Silu), reads=(bzB,), writes=(szbB[j],))

    def mixerB_conv(sg, j):
        T = sg.T
        bk, bkB = nbank()
        S.op("pe", lambda e: mm_group(e, sg.v(bk[:, 0:T]), [(diag[:, j, k, :], sg.tap(glu, j, 30, k)) for k in range(31)]),
             reads=(diagBj[j], gluB[j]), writes=(bkB,))
        S.op("act", lambda e: e.activation(out=xb[:, j, 0:T], in_=bk[:, 0:T], func=AF.Identity, bias=biasv(j)), reads=(bkB, parT), writes=(xbB[j],))
        S.op("act", lambda e: e.activation(out=sqb[:, j, 0:T], in_=bk[:, 0:T], func=AF.Square, bias=biasv(j)), reads=(bkB, parT), writes=(sqbB[j],))
        S.op("dve", lambda e: e.tensor_copy(out=xbb[:, j, 0:T], in_=xb[:, j, 0:T]), reads=(xbB[j],), writes=(xbbB[j],))

    def mixerB_stats(sg):
        T = sg.T
        bm, bmB = nbank()
        S.op("pe", lambda e: mm_group(e, bm[:, 0:T], [(c512[:], xbb[:, j, 0:T]) for j in range(4)]), reads=tuple(xbbB) + (constB,), writes=(bmB,))
        bq, bqB = nbank()
        S.op("pe", lambda e: mm_group(e, bq[:, 0:T], [(c512[:], sqb[:, j, 0:T]) for j in range(4)]), reads=tuple(sqbB) + (constB,), writes=(bqB,))
        mean, meanB = lnm, lnmB
        S.op("act", lambda e: e.activation(out=mean[:, 0:T], in_=bm[:, 0:T], func=AF.Copy), reads=(bmB,), writes=(meanB,))
        m2, m2B = ntmp()
        S.op("dve", lambda e: e.tensor_tensor(out=m2[:, 0:T], in0=mean[:, 0:T], in1=mean[:, 0:T], op=ALU.mult), reads=(meanB,), writes=(m2B,))
        var, varB = ntmp()
        S.op("dve", lambda e: e.scalar_tensor_tensor(out=var[:, 0:T], in0=bq[:, 0:T], scalar=EPS, in1=m2[:, 0:T], op0=ALU.add, op1=ALU.subtract), reads=(bqB, m2B), writes=(varB,))
        rs, rsB = lnr, lnrB
        sd, sdB = ntmp()
        S.op("act", lambda e: e.activation(out=sd[:, 0:T], in_=var[:, 0:T], func=AF.Sqrt), reads=(varB,), writes=(sdB,))
        S.op("dve", lambda e: e.reciprocal(out=rs[:, 0:T], in_=sd[:, 0:T]), reads=(sdB,), writes=(rsB,))
        return mean, meanB, rs, rsB

    def mixerB_back_a(sg, j, mean, meanB, rs, rsB):
        T = sg.T
        xc, xcB = ntmp()
        S.op("pool", lambda e: e.tensor_tensor(out=xc[:, 0:T], in0=xb[:, j, 0:T], in1=mean[:, 0:T], op=ALU.subtract), reads=(xbB[j], meanB), writes=(xcB,))
        xn, xnB = ntmp()
        S.op("dve", lambda e: e.tensor_tensor(out=xn[:, 0:T], in0=xc[:, 0:T], in1=rs[:, 0:T], op=ALU.mult), reads=(xcB, rsB), writes=(xnB,))
        s_, sB = ntmp()
        S.op("act", lambda e: e.activation(out=s_[:, 0:T], in_=xn[:, 0:T], func=AF.Silu, bias=betv(j), scale=gamv(j)), reads=(xnB, parT), writes=(sB,))
        return s_, sB

    def mixerB_back_b(sg, j, s_, sB):
        T = sg.T
        S.op("dve", lambda e: e.tensor_tensor(out=ycat[:, 4 + j, 0:T], in0=s_[:, 0:T], in1=szb[:, j, 0:T], op=ALU.mult), reads=(sB, szbB[j]), writes=(ycatB[4 + j],))

    def mixerA(sg, j, grp):
        T = sg.T
        (lbg, wbgB), (lcg, wcgB), (lx, wxB), (laz, wazB) = grp
        cg, cgB = proj_chunk(sg, lcg, wcgB, j * 128)
        ax, axB = proj_chunk(sg, lx, wxB, j * 128)
        bgk, bgkB = proj_chunk(sg, lbg, wbgB, j * 128)
        az, azB = proj_chunk(sg, laz, wazB, j * 128)
        cgs, cgsB = ntmp()
        S.op("act", lambda e: e.activation(out=cgs[:, 0:T], in_=cg[:, 0:T], func=AF.Copy), reads=(cgB,), writes=(cgsB,))
        S.op("dve", lambda e: e.tensor_tensor(out=sg.new(cxe, j, 2), in0=sg.v(ax[:, 0:T]), in1=sg.v(cgs[:, 0:T]), op=ALU.mult), reads=(axB, cgsB), writes=(cxB[j],))
        a0, a0B = ntmp()
        S.op("dve", lambda e: e.tensor_scalar(out=sg.v(a0[:, 0:T]), in0=sg.tap(cxe, j, 2, 0), scalar1=cawv(j, 0), scalar2=None, op0=ALU.mult), reads=(cxB[j], parT), writes=(a0B,))
        a1, a1B = ntmp()
        S.op("dve", lambda e: e.scalar_tensor_tensor(out=sg.v(a1[:, 0:T]), in0=sg.tap(cxe, j, 2, 1), scalar=cawv(j, 1), in1=sg.v(a0[:, 0:T]), op0=ALU.mult, op1=ALU.add),
             reads=(cxB[j], parT, a0B), writes=(a1B,))
        a2, a2B = ntmp()
        S.op("dve", lambda e: e.scalar_tensor_tensor(out=sg.v(a2[:, 0:T]), in0=sg.tap(cxe, j, 2, 2), scalar=cawv(j, 2), in1=sg.v(a1[:, 0:T]), op0=ALU.mult, op1=ALU.add),
             reads=(cxB[j], parT, a1B), writes=(a2B,))
        sz, szB = ntmp()
        S.op("act", lambda e: e.activation(out=sz[:, 0:T], in_=az[:, 0:T], func=AF.Silu), reads=(azB,), writes=(szB,))
        t1, t1B = ntmp()
        S.op("dve", lambda e: e.tensor_tensor(out=t1[:, 0:T], in0=bgk[:, 0:T], in1=a2[:, 0:T], op=ALU.mult), reads=(bgkB, a2B), writes=(t1B,))
        S.op("dve", lambda e: e.tensor_tensor(out=ycat[:, j, 0:T], in0=t1[:, 0:T], in1=sz[:, 0:T], op=ALU.mult), reads=(t1B, szB), writes=(ycatB[j],))

    def out_proj(sg, names, korder):
        T = sg.T
        for h in range(2):
            lhs, wB = use_block(names[h])
            pm_ = proj_multi(sg, [(lhs, wB, q_ * 128) for q_ in range(4)], src=ycat, srcB=ycatB, korder=korder) if h == 0 else None
            for q in range(4):
                oc = h * 4 + q
                bk, bkB = pm_[q] if pm_ is not None else proj_chunk(sg, lhs, wB, q * 128, src=ycat, srcB=ycatB, kcs=korder)
                S.op("dve", lambda e, oc=oc, bk=bk: e.tensor_tensor(out=hT[:, oc, 0:T], in0=bk[:, 0:T], in1=hT[:, oc, 0:T], op=ALU.add), reads=(bkB, hTB[oc]), writes=(hTB[oc],))
                S.op("act", lambda e, oc=oc: e.activation(out=hbf[:, oc, 0:T], in_=hT[:, oc, 0:T], func=AF.Copy), reads=(hTB[oc],), writes=(hbfB[oc],))

    def ple(sg, l):
        T = sg.T
        pend = None
        for h in range(2):
            (lp, wpB), (lg, wgB) = use_group([f"Wp{l}{h}", f"Wg{l}{h}"])
            for q in range(4):
                oc = h * 4 + q
                bg, bgB = proj_chunk(sg, lg, wgB, q * 128)
                bp, bpB = nbank()
                S.op("pe", lambda e, bp=bp, q=q, lp=lp: mm_group(e, bp[:, 0:T], [(lp(k, q * 128), pT[:, l * 2 + k, 0:T]) for k in range(2)]),
                     reads=(wpB, pTB[l]), writes=(bpB,))
                th, thB = ntmp()
                S.op("act", lambda e, th=th, bg=bg: e.activation(out=th[:, 0:T], in_=bg[:, 0:T], func=AF.Tanh, scale=0.5), reads=(bgB,), writes=(thB,))
                if pend is not None:
                    S.op("act", lambda e, oc=pend: e.activation(out=sq[:, oc, 0:T], in_=hT[:, oc, 0:T], func=AF.Square), reads=(hTB[pend],), writes=(sqB[pend],))
                    if l == 0:
                        S.op("act", lambda e, oc=pend: e.activation(out=ycat[:, oc, 0:T], in_=hT[:, oc, 0:T], func=AF.Copy, scale=g32[:, oc, 1:2]),
                             reads=(hTB[pend], parT), writes=(ycatB[pend],))
                t, tB = ntmp()
                S.op("dve", lambda e, t=t, th=th, bp=bp: e.scalar_tensor_tensor(out=t[:, 0:T], in0=th[:, 0:T], scalar=1.0, in1=bp[:, 0:T], op0=ALU.add, op1=ALU.mult),
                     reads=(thB, bpB), writes=(tB,))
                S.op("dve", lambda e, t=t, oc=oc: e.scalar_tensor_tensor(out=hT[:, oc, 0:T], in0=t[:, 0:T], scalar=0.5, in1=hT[:, oc, 0:T], op0=ALU.mult, op1=ALU.add),
                     reads=(tB, hTB[oc]), writes=(hTB[oc],))
                pend = oc
        S.op("act", lambda e, oc=pend: e.activation(out=sq[:, oc, 0:T], in_=hT[:, oc, 0:T], func=AF.Square), reads=(hTB[pend],), writes=(sqB[pend],))
        if l == 0:
            S.op("act", lambda e, oc=pend: e.activation(out=ycat[:, oc, 0:T], in_=hT[:, oc, 0:T], func=AF.Copy, scale=g32[:, oc, 1:2]),
                 reads=(hTB[pend], parT), writes=(ycatB[pend],))

    def layer0(sg, last):
        rrp = layer_in(sg, 0)
        gB = use_group(["Sv", "Sg", "Sz"])
        mixerB_front(sg, 0, last, gB, rrp)
        mixerB_front(sg, 1, last, gB, rrp)
        mixerB_front(sg, 2, last, gB)
        mixerB_conv(sg, 0)
        mixerB_front(sg, 3, last, gB)
        mixerB_conv(sg, 1)
        mixerB_conv(sg, 2)
        mixerB_conv(sg, 3)
        gA = use_group(["Sbg", "Scg", "Sx", "Saz"])
        mixerA(sg, 0, gA)
        st = mixerB_stats(sg)
        prev = None
        for j in range(4):
            cur = mixerB_back_a(sg, j, *st)
            if prev is not None:
                mixerB_back_b(sg, j - 1, *prev)
            prev = cur
        mixerB_back_b(sg, 3, *prev)
        for j in range(1, 4):
            mixerA(sg, j, gA)
        if DBG == "ycat":
            return
        out_proj(sg, ("Woe0", "Woe1"), [4, 5, 6, 7, 0, 1, 2, 3])
        if DBG == "mixer":
            return
        ple(sg, 0)

    def layer1(sg, first, hook=None):
        T = sg.T
        L = 15 + 512 if sg.kind == "P" else 16 * 23
        rr1, rr1B = layer_in(sg, 1)
        for h in range(2):
            lu, wuB = use_block(f"Bu{h}")
            ubk = proj_multi(sg, [(lu, wuB, q_ * 128) for q_ in range(4)], src=ycat, srcB=ycatB) if h == 0 else None
            for q in range(4):
                c = h * 4 + q
                bk, bkB = ubk[q] if ubk is not None else proj_chunk(sg, lu, wuB, q * 128)
                if ubk is not None:
                    S.op("dve", lambda e, bk=bk, c=c: e.tensor_tensor(out=sg.new(ue, c, 15), in0=sg.v(bk[:, 0:T]), in1=sg.v(rr1[:, 0:T]), op=ALU.mult),
                         reads=(bkB, rr1B), writes=(ueB[c],))
                else:
                    S.op("act", lambda e, bk=bk, c=c: e.activation(out=sg.new(ue, c, 15), in_=sg.v(bk[:, 0:T]), func=AF.Copy), reads=(bkB,), writes=(ueB[c],))
            lz, wzB = use_block(f"Bz{h}")
            for q in range(4):
                c = h * 4 + q
                g = c // 2
                w = 2 << g
                cur, curB, lo = ue[:, c, :], ueB[c], 0
                for lvl in range(g + 1):
                    sh = 1 << lvl
                    nlo = lo + sh
                    nt_, ntB = ntmp()
                    S.op("pool" if c < 4 else "dve", lambda e, nt_=nt_, cur=cur, nlo=nlo, sh=sh: e.tensor_tensor(out=nt_[:, nlo:L], in0=cur[:, nlo:L], in1=cur[:, nlo - sh:L - sh], op=ALU.add),
                         reads=(curB,), writes=(ntB,))
                    cur, curB, lo = nt_, ntB, nlo
                if sg.kind == "P":
                    wn = cur[:, 15:15 + 512]
                else:
                    wn = cur[:, 0:16 * 23].rearrange("p (s l) -> p s l", l=23)[:, :, 15:23]
                S.op("dve", lambda e, wn=wn, c=c, w=w: e.scalar_tensor_tensor(out=sg.v(ycat[:, c, 0:T]), in0=wn, scalar=1.0 / float(w), in1=sg.new(ue, c, 15), op0=ALU.mult, op1=ALU.subtract),
                     reads=(curB, ueB[c]), writes=(ycatB[c],))
                if first:
                    f1, f1B = ntmp()
                    S.op("dve", lambda e, f1=f1, cur=cur, g=g: e.tensor_tensor(out=f1[:, 0:16], in0=cur[:, 15:31], in1=cnt_t[:, g * 16:(g + 1) * 16], op=ALU.mult),
                         reads=(curB, cstB), writes=(f1B,))
                    S.op("dve", lambda e, f1=f1, c=c: e.tensor_tensor(out=ycat[:, c, 0:16], in0=f1[:, 0:16], in1=ue[:, c, 15:31], op=ALU.subtract),
                         reads=(f1B, ueB[c]), writes=(ycatB[c],))
                bk, bkB = proj_chunk(sg, lz, wzB, q * 128)
                S.op("act", lambda e, bk=bk, q=q: e.activation(out=szb[:, q, 0:T], in_=bk[:, 0:T], func=AF.Silu), reads=(bkB,), writes=(szbB[q],))
            lpm, pmB = use_block(f"Pm{h}")
            for q in range(4):
                c = h * 4 + q
                g, hh = c // 2, c % 2
                bk, bkB = nbank()
                S.op("pe", lambda e, bk=bk, g=g, hh=hh, h=h, lpm=lpm: mm_group(e, bk[:, 0:T], [(lpm((g - 2 * h) * 2 + k2, hh * 128), ycat[:, 2 * g + k2, 0:T]) for k2 in range(2)]),
                     reads=(pmB, ycatB[2 * g], ycatB[2 * g + 1]), writes=(bkB,))
                S.op("dve", lambda e, bk=bk, c=c, q=q: e.scalar_tensor_tensor(out=sq[:, c, 0:T], in0=bk[:, 0:T], scalar=pscv(c), in1=szb[:, q, 0:T], op0=ALU.mult, op1=ALU.mult),
                     reads=(bkB, szbB[q], parT), writes=(sqB[c],))
        if hook is not None:
            hook()
        out_proj_l1(sg)
        ple(sg, 1)

    def out_proj_l1(sg):
        T = sg.T
        for h in range(2):
            lhs, wB = use_block(f"Woo{h}")
            pm_ = proj_multi(sg, [(lhs, wB, q_ * 128) for q_ in range(4)], src=sq, srcB=sqB) if h == 0 else None
            for q in range(4):
                oc = h * 4 + q
                bk, bkB = pm_[q] if pm_ is not None else proj_chunk(sg, lhs, wB, q * 128, src=sq, srcB=sqB)
                S.op("dve", lambda e, oc=oc, bk=bk: e.tensor_tensor(out=hT[:, oc, 0:T], in0=bk[:, 0:T], in1=hT[:, oc, 0:T], op=ALU.add), reads=(bkB, hTB[oc]), writes=(hTB[oc],))
                S.op("act", lambda e, oc=oc: e.activation(out=hbf[:, oc, 0:T], in_=hT[:, oc, 0:T], func=AF.Copy), reads=(hTB[oc],), writes=(hbfB[oc],))

    hbf32 = hbf.bitcast(F32)[:].rearrange("p a b -> p (a b)").rearrange("p (c t) -> p c t", t=512)
    ycat32 = ycat.bitcast(F32)[:].rearrange("p a b -> p (a b)").rearrange("p (c t) -> p c t", t=512)

    hu0 = xb.bitcast(BF16)[:].rearrange("p a b -> p (a b)").rearrange("p (c t) -> p c t", t=512)
    hu0B = [xbB[c // 2] for c in range(8)]

    def yn_view(c):
        if c < 4:
            return hbf32[:, c, :], (hbfB[2 * c], hbfB[2 * c + 1])
        return ycat32[:, c - 4, :], (ycatB[2 * (c - 4)], ycatB[2 * (c - 4) + 1])

    rtok = sb("rtok", [128, 8]); rtokB = Buf("rtok")

    def final_a(sg):
        T = sg.T
        bk, bkB = nbank()
        subs = []
        for sub in range(sg.nsub):
            for c in range(8):
                subs.append((lambda e, sub=sub, c=c: e.matmul(bk[:, sub:sub + 1], lhsT=sq[:, c, sub * 128:(sub + 1) * 128], rhs=onesb[:, 0:1],
                                                            start=(c == 0), stop=(c == 7)), (sqB[c], constB)))
        S.group("pe", subs, writes=(bkB,))
        S.op("act", lambda e: e.activation(out=rtok[:, 4:4 + sg.nsub], in_=bk[:, 0:sg.nsub], func=AF.Sqrt, bias=EPS, scale=1.0 / 1024.0), reads=(bkB,), writes=(rtokB,))
        S.op("dve", lambda e: e.reciprocal(out=rtok[:, 0:sg.nsub], in_=rtok[:, 4:4 + sg.nsub]), reads=(rtokB,), writes=(rtokB,))
        for c in range(8):
            yv, yB = yn_view(c)
            S.op("dve", lambda e, c=c, yv=yv: e.tensor_scalar(out=yv[:, 0:T], in0=hT[:, c, 0:T], scalar1=g32[:, c, 2:3], scalar2=None, op0=ALU.mult),
                 reads=(hTB[c], parT), writes=yB)

    def final_b(sg):
        for sub in range(sg.nsub):
            io = nio()
            for half in range(2):
                bk, bkB = nbank()

                def _tp(e, bk=bk, half=half, sub=sub):
                    for q in range(4):
                        yv, _ = yn_view(half * 4 + q)
                        i = e.transpose(out=bk[:, q * 128:(q + 1) * 128], in_=yv[:, sub * 128:(sub + 1) * 128], identity=ident)
                    return i
                rb = ()
                for q in range(4):
                    rb = rb + yn_view(half * 4 + q)[1]
                S.op("pe", _tp, reads=rb + (cstB,), writes=(bkB,))
                if half == 0:
                    S.op("act", lambda e, bk=bk, io=io, sub=sub: e.activation(out=xio[io][:, 0:512], in_=bk[:, :], func=AF.Copy, scale=rtok[:, sub:sub + 1]),
                         reads=(bkB, rtokB), writes=(xioB[io],))
                else:
                    S.op("dve", lambda e, bk=bk, io=io, sub=sub: e.tensor_scalar(out=xio[io][:, 512:1024], in0=bk[:, :], scalar1=rtok[:, sub:sub + 1], scalar2=None, op0=ALU.mult),
                         reads=(bkB, rtokB), writes=(xioB[io],))
            if sg.kind == "P":
                r0 = sg.ti * 512 + sub * 128
                dst = y_p[r0:r0 + 128, :]
            else:
                dst = y_s
            iodma(f"xio{io}", lambda e, io=io, dst=dst: e.dma_start(out=dst, in_=xio[io][:]), reads=(xioB[io],))

    def halo_shift(sg):
        S.op("dve", lambda e: e.tensor_copy(out=cxe[:, :, 0:2], in_=cxe[:, :, 512:514]), writes=tuple(cxB))
        S.op("dve", lambda e: e.tensor_copy(out=glu[:, :, 0:30], in_=glu[:, :, 512:542]), writes=tuple(gluB))
        S.op("dve", lambda e: e.tensor_copy(out=ue[:, :, 0:15], in_=ue[:, :, 512:527]), writes=tuple(ueB))

    def state_out_prompt(sg):
        io0 = nio()
        sst, sstB = xio[io0], xioB[io0]
        bk, bkB = nbank()

        def _t1(e, bk=bk):
            for j in range(4):
                i = e.transpose(out=bk[0:32, j * 128:(j + 1) * 128], in_=cxe[:, j, 482:514], identity=ident)
            return i
        S.op("pe", _t1, reads=tuple(cxB) + (cstB,), writes=(bkB,))
        S.op("act", lambda e, bk=bk, sst=sst: e.activation(out=sst[0:32, 0:512], in_=bk[0:32, :], func=AF.Copy), reads=(bkB,), writes=(sstB,))
        S.dma("sp", "stp_a", lambda e, sst=sst: e.dma_start(out=na_p, in_=sst[30:32, 0:512]), reads=(sstB,))
        bk2, bk2B = nbank()

        def _t2(e, bk2=bk2):
            for j in range(4):
                i = e.transpose(out=bk2[0:32, j * 128:(j + 1) * 128], in_=gl32[:, j, 0:32], identity=ident)
            return i
        S.op("pe", _t2, reads=tuple(gl32B) + (cstB,), writes=(bk2B,))
        S.op("act", lambda e, bk2=bk2, sst=sst: e.activation(out=sst[0:32, 512:1024], in_=bk2[0:32, :], func=AF.Copy, scale=0.5), reads=(bk2B,), writes=(sstB,))
        S.dma("sp", "stp_b", lambda e, sst=sst: e.dma_start(out=nb_p, in_=sst[2:32, 512:1024]), reads=(sstB,))
        io = nio()
        for half in range(2):
            bk3, bk3B = nbank()

            def _t3(e, bk3=bk3, half=half):
                for q in range(4):
                    i = e.transpose(out=bk3[0:32, q * 128:(q + 1) * 128], in_=ue[:, half * 4 + q, 495:527], identity=ident)
                return i
            S.op("pe", _t3, reads=tuple(ueB[half * 4:half * 4 + 4]) + (cstB,), writes=(bk3B,))
            S.op("act", lambda e, bk3=bk3, half=half, io=io: e.activation(out=xio[io][0:32, half * 512:(half + 1) * 512], in_=bk3[0:32, :], func=AF.Copy), reads=(bk3B,), writes=(xioB[io],))
        S.dma("sp", "stp_c", lambda e, io=io: e.dma_start(out=np_p, in_=xio[io][17:32, :]), reads=(xioB[io],))

    def state_in_sample():
        stc = {"i": 0}

        def nst():
            i = nio()
            return xio[i], xioB[i], f"xio{i}"
        sst, sstB, skey = nst()
        iodma(skey, lambda e, sst=sst: e.dma_start(out=sst[0:32, 0:512], in_=sta), writes=(sstB,))
        bk, bkB = nbank()

        def _t1(e, bk=bk, sst=sst):
            for j in range(4):
                i = e.transpose(out=bk[:, j * 32:(j + 1) * 32], in_=sst[0:32, j * 128:(j + 1) * 128], identity=ident[0:32, 0:32])
            return i
        S.op("pe", _t1, reads=(sstB, cstB), writes=(bkB,))
        S.op("dve", lambda e, bk=bk: e.tensor_copy(out=cxe[:, :, 0:160].rearrange("p j (s l) -> p j s l", l=10)[:, :, :, 0:2],
                                                   in_=bk[:, 0:128].rearrange("p (j s r) -> p j s r", s=16, r=2)),
             reads=(bkB,), writes=tuple(cxB))
        for rb in range(4):
            sst, sstB, skey = nst()
            iodma(skey, lambda e, rb=rb, sst=sst: e.dma_start(out=sst[0:120, 0:512], in_=stb[rb * 4:(rb + 1) * 4].rearrange("s r f -> (s r) f")), writes=(sstB,))
            bk, bkB = nbank()

            def _t2(e, bk=bk, sst=sst):
                for j in range(4):
                    i = e.transpose(out=bk[:, j * 120:(j + 1) * 120], in_=sst[0:120, j * 128:(j + 1) * 128], identity=ident[0:120, 0:120])
                return i
            S.op("pe", _t2, reads=(sstB, cstB), writes=(bkB,))
            S.op("dve", lambda e, bk=bk, rb=rb: e.tensor_scalar(
                out=glu[:, :, 0:608].rearrange("p j (s l) -> p j s l", l=38)[:, :, rb * 4:(rb + 1) * 4, 0:30],
                in0=bk[:, 0:480].rearrange("p (j s r) -> p j s r", s=4, r=30), scalar1=2.0, scalar2=None, op0=ALU.mult),
                reads=(bkB,), writes=tuple(gluB))
        for rb in range(2):
            sst, sstB, skey = nst()
            iodma(skey, lambda e, rb=rb, sst=sst: e.dma_start(out=sst[0:120, :], in_=stp[rb * 8:(rb + 1) * 8].rearrange("s r f -> (s r) f")), writes=(sstB,))
            for half in range(2):
                bk, bkB = nbank()

                def _t3(e, bk=bk, half=half, sst=sst):
                    for q in range(4):
                        c = half * 4 + q
                        i = e.transpose(out=bk[:, q * 120:(q + 1) * 120], in_=sst[0:120, c * 128:(c + 1) * 128], identity=ident[0:120, 0:120])
                    return i
                S.op("pe", _t3, reads=(sstB, cstB), writes=(bkB,))
                S.op("dve", lambda e, bk=bk, rb=rb, half=half: e.tensor_copy(
                    out=ue[:, half * 4:half * 4 + 4, 0:368].rearrange("p c (s l) -> p c s l", l=23)[:, :, rb * 8:(rb + 1) * 8, 0:15],
                    in_=bk[:, 0:480].rearrange("p (c s r) -> p c s r", s=8, r=15)),
                    reads=(bkB,), writes=tuple(ueB[half * 4:half * 4 + 4]))

    def state_out_sample(sg):
        iodma("st_out", lambda e: e.dma_start(out=nb_s[:, 0:22, :], in_=stb[:, 8:30, :]))
        iodma("st_out", lambda e: e.dma_start(out=np_s[:, 0:7, :], in_=stp[:, 8:15, :]))
        c1, c1B = ntmp()
        S.op("dve", lambda e: e.tensor_copy(out=c1[:, 0:128].rearrange("p (j s r) -> p j s r", s=16, r=2),
                                            in_=cxe[:, :, 0:160].rearrange("p j (s l) -> p j s l", l=10)[:, :, :, 8:10]),
             reads=tuple(cxB), writes=(c1B,))
        bk, bkB = nbank()

        def _t1(e, bk=bk):
            for j in range(4):
                i = e.transpose(out=bk[0:32, j * 128:(j + 1) * 128], in_=c1[:, j * 32:(j + 1) * 32], identity=ident)
            return i
        S.op("pe", _t1, reads=(c1B, cstB), writes=(bkB,))
        io = nio()
        S.op("act", lambda e, bk=bk, io=io: e.activation(out=xio[io][0:32, 0:512], in_=bk[0:32, :], func=AF.Copy), reads=(bkB,), writes=(xioB[io],))
        iodma(f"xio{io}", lambda e, io=io: e.dma_start(out=na_s, in_=xio[io][0:32, 0:512]), reads=(xioB[io],))
        bk, bkB = nbank()

        def _t2(e, bk=bk):
            for j in range(4):
                i = e.transpose(out=bk[:, j * 128:(j + 1) * 128], in_=gl32[:, j, 0:128], identity=ident)
            return i
        S.op("pe", _t2, reads=tuple(gl32B) + (cstB,), writes=(bkB,))
        io = nio()
        S.op("act", lambda e, bk=bk, io=io: e.activation(out=xio[io][:, 0:512], in_=bk[:, :], func=AF.Copy, scale=0.5), reads=(bkB,), writes=(xioB[io],))
        for s in range(16):
            iodma(f"xio{io}", lambda e, io=io, s=s: e.dma_start(out=nb_s[s, 22:30, :], in_=xio[io][s * 8:(s + 1) * 8, 0:512]), reads=(xioB[io],))
        io = nio()
        for half in range(2):
            c2, c2B = ntmp()
            S.op("dve", lambda e, c2=c2, half=half: e.tensor_copy(
                out=c2[:, 0:512].rearrange("p (c s t) -> p c s t", s=16, t=8),
                in_=ue[:, half * 4:half * 4 + 4, 0:368].rearrange("p c (s l) -> p c s l", l=23)[:, :, :, 15:23]),
                reads=tuple(ueB[half * 4:half * 4 + 4]), writes=(c2B,))
            bk, bkB = nbank()

            def _t3(e, bk=bk, c2=c2):
                for q in range(4):
                    i = e.transpose(out=bk[:, q * 128:(q + 1) * 128], in_=c2[:, q * 128:(q + 1) * 128], identity=ident)
                return i
            S.op("pe", _t3, reads=(c2B, cstB), writes=(bkB,))
            S.op("act", lambda e, bk=bk, io=io, half=half: e.activation(out=xio[io][:, half * 512:(half + 1) * 512], in_=bk[:, :], func=AF.Copy), reads=(bkB,), writes=(xioB[io],))
        for s in range(16):
            iodma(f"xio{io}", lambda e, io=io, s=s: e.dma_start(out=np_s[s, 7:15, :], in_=xio[io][s * 8:(s + 1) * 8, :]), reads=(xioB[io],))

    segs = [Seg("P", t) for t in range(NPT)] + [Seg("S", NPT)]

    def enter(sg):
        S.ctx = f"{sg.kind}{sg.ti}"
        st_["ioq"] = "sp" if sg.ti == 0 else "pool"
    enter(segs[0])
    load_tile(segs[0])
    make_hu0(segs[0])
    for si_, sg in enumerate(segs):
        enter(sg)
        lastP = (sg.kind == "P" and sg.ti == NPT - 1)
        layer0(sg, lastP or sg.kind == "S")
        if DBG in ("ycat", "mixer", "L0"):
            break
        layer1(sg, sg.kind == "P" and sg.ti == 0, hook=(lambda sg=sg: state_out_prompt(sg)) if lastP else None)
        if DBG == "L1":
            break
        if sg.kind == "S":
            state_out_sample(sg)
        if sg.kind == "P" and not lastP:
            halo_shift(sg)
        final_a(sg)
        if si_ + 1 < len(segs):
            nx = segs[si_ + 1]
            enter(nx)
            if nx.kind == "S":
                state_in_sample()
            load_tile(nx)
            make_hu0(nx)
            enter(sg)
        final_b(sg)
    S.final_wait("sp")

    sems = {k: es.enter_context(nc.semaphore(f"sem_{k}")) for k in S.sem_keys()}
    block = es.enter_context(nc.Block())

    @block.tensor
    def _(e):
        S.replay("pe", e, sems)

    @block.scalar
    def _(e):
        S.replay("act", e, sems)

    @block.vector
    def _(e):
        S.replay("dve", e, sems)

    @block.gpsimd
    def _(e):
        S.replay("pool", e, sems)

    @block.sync
    def _(e):
        S.replay("sp", e, sems)

    es.close()
    nc._sched = S
    return nc


def make_consts():
    c = np.zeros((128, 192), np.float32)
    c[:, 0:128] = np.eye(128, dtype=np.float32)
    for g, w in enumerate((2, 4, 8, 16)):
        c[:, 128 + g * 16:128 + (g + 1) * 16] = (1.0 / np.minimum(np.arange(16) + 1, w)).astype(np.float32)[None, :]
    return c


def core_inputs(i, NPT, x_prompt, x_sample, state_conv_a, state_conv_b, state_pool, p_prompt, p_sample,
                norm_g, w_in_even, conv_a_w, conv_b_w, conv_b_bias, ln_b_gamma, ln_b_beta, w_out_even,
                w_in_odd, pool_map, pool_scale, w_out_odd, ple_proj, ple_gate, final_norm_g):
    f = lambda a: np.ascontiguousarray(a, dtype=np.float32)
    s0 = i * 16
    PA = np.concatenate([conv_a_w[0], conv_b_w[0], conv_b_bias, ln_b_gamma, ln_b_beta], axis=0)
    PB = np.concatenate([norm_g, pool_scale, final_norm_g[None, :]], axis=0)
    return {
        "xp": f(x_prompt[i]), "xs": f(x_sample[s0:s0 + 16].reshape(128, 1024)),
        "sta": f(state_conv_a[0, s0:s0 + 16].reshape(32, 512)), "stb": f(state_conv_b[0, s0:s0 + 16]),
        "stp": f(state_pool[0, s0:s0 + 16]),
        "pp": f(p_prompt[:, i]), "psm": f(p_sample[:, s0:s0 + 16].reshape(2, 128, 256)),
        "PA": f(PA), "PB": f(PB),
        "w_ie": f(w_in_even[0]), "w_oe": f(w_out_even[0]), "w_io": f(w_in_odd[0]),
        "pmap": f(pool_map[0].reshape(1024, 256)), "w_oo": f(w_out_odd[0]),
        "plp": f(ple_proj.reshape(512, 1024)), "plg": f(ple_gate.reshape(2048, 1024)),
        "cst": make_consts(),
    }


_NC_CACHE = {}


def kernel(**inputs):
    inputs = {k: np.asarray(v) for k, v in inputs.items()}
    NPT = inputs["x_prompt"].shape[1] // 512
    if NPT not in _NC_CACHE:
        _NC_CACHE[NPT] = build(NPT)
    nc = _NC_CACHE[NPT]
    in_maps = [core_inputs(i, NPT, **inputs) for i in range(NCORES)]
    res = run_bass_kernel_spmd(nc, in_maps, core_ids=list(range(NCORES)))
    R = res.results
    SEQ = NPT * 512
    y_prompt = np.stack([R[i]["y_p"] for i in range(NCORES)], 0).reshape(NCORES, SEQ, 1024)
    y_sample = np.concatenate([R[i]["y_s"].reshape(16, 8, 1024) for i in range(NCORES)], 0)
    na_p = np.stack([R[i]["na_p"] for i in range(NCORES)], 0)[None]
    nb_p = np.stack([R[i]["nb_p"] for i in range(NCORES)], 0)[None]
    np_p = np.stack([R[i]["np_p"] for i in range(NCORES)], 0)[None]
    na_s = np.concatenate([R[i]["na_s"].reshape(16, 2, 512) for i in range(NCORES)], 0)[None]
    nb_s = np.concatenate([R[i]["nb_s"] for i in range(NCORES)], 0)[None]
    np_s = np.concatenate([R[i]["np_s"] for i in range(NCORES)], 0)[None]
    f = lambda a: np.ascontiguousarray(a, dtype=np.float32)
    return tuple(f(a) for a in (y_prompt, y_sample, na_p, nb_p, np_p, na_s, nb_s, np_s))
```
